# Optimizing a Trainium2 kernel written in Bass

```python
import jax, jax.numpy as jnp
from jax import lax
import numpy as np

D_MODEL = 1024
BATCH = 8
SEQ = 2048
DEPTH = 2
DEC_BATCH = 128
DEC_SEQ = 8
PAST_LEN = 16384
PAGE_SIZE = 128

N_MIXERS = 2
N_CONV_LAYERS = (DEPTH + 1) // 2
N_GDN_LAYERS = DEPTH // 2
D_CONV = D_MODEL
CONV_A_WIDTH = 3
GDN_HEADS = 8
GDN_DK = 128
GDN_DV = 128
GDN_QK = GDN_HEADS * GDN_DK
GDN_VW = GDN_HEADS * GDN_DV
GDN_CONV_CH = 2 * GDN_QK + GDN_VW
GDN_CONV_WIDTH = 4
GDN_IN = GDN_CONV_CH + GDN_VW + 2 * GDN_HEADS
GDN_CHUNK = 64
ALPHA = (2 * DEPTH) ** 0.25
BETA_INIT = (8 * DEPTH) ** -0.25
LN_EPS = 1e-5
NORM_EPS = 1e-6

kernel_name = 'hybrid_shortconv_gdn_adaln_deepnorm_step'


def causal_dwconv(x, buf, w):
    width = w.shape[0]
    T = x.shape[1]
    xp = jnp.concatenate([buf.astype(x.dtype), x], axis=1)
    y = xp[:, 0:T] * w[0]
    for j in range(1, width):
        y = y + xp[:, j:j + T] * w[j]
    return y, xp[:, T:]


def layer_norm(x, g, b):
    xf = x.astype(jnp.float32)
    mu = jnp.mean(xf, axis=-1, keepdims=True)
    var = jnp.mean(jnp.square(xf - mu), axis=-1, keepdims=True)
    return ((xf - mu) * lax.rsqrt(var + LN_EPS) * g.astype(jnp.float32) + b.astype(jnp.float32)).astype(x.dtype)


def l2norm(x):
    return x * lax.rsqrt(jnp.sum(x * x, axis=-1, keepdims=True) + NORM_EPS)


def gated_delta_rule(q, k, v, g, beta, s0):
    bsz, T, H, dk = q.shape
    dv = v.shape[-1]
    C = min(GDN_CHUNK, T)
    n = -(-T // C)
    pad = n * C - T

    def prep(t):
        t = jnp.pad(t, [(0, 0), (0, pad)] + [(0, 0)] * (t.ndim - 2))
        t = t.reshape((bsz, n, C) + t.shape[2:])
        return jnp.moveaxis(t, 3, 1)

    q, k, v, g, beta = [prep(t) for t in (q * dk ** -0.5, k, v, g, beta)]
    G = jnp.cumsum(g, axis=-1)
    idx = jnp.arange(C)
    causal = idx[:, None] >= idx[None, :]
    strict = idx[:, None] > idx[None, :]
    diff = G[..., :, None] - G[..., None, :]
    L = jnp.where(causal, jnp.exp(jnp.where(causal, diff, 0.0)), 0.0)
    kb = k * beta[..., None]
    A = jnp.where(strict, jnp.einsum('bhnid,bhnjd->bhnij', kb, k) * L, 0.0)
    IA = A + jnp.eye(C, dtype=A.dtype)
    rhs = jnp.concatenate([v * beta[..., None], kb * jnp.exp(G)[..., None]], axis=-1)
    sol = lax.linalg.triangular_solve(IA, rhs, left_side=True, lower=True, unit_diagonal=True)
    u, w = sol[..., :dv], sol[..., dv:]
    attn = jnp.einsum('bhnid,bhnjd->bhnij', q, k) * L
    qg = q * jnp.exp(G)[..., None]
    kt = k * jnp.exp(G[..., -1:] - G)[..., None]
    gl = jnp.exp(G[..., -1])
    xs = tuple(jnp.moveaxis(t, 2, 0) for t in (u, w, qg, attn, kt, gl))

    def step(S, inp):
        u_c, w_c, qg_c, attn_c, kt_c, gl_c = inp
        v_new = u_c - jnp.einsum('bhck,bhkv->bhcv', w_c, S)
        o = jnp.einsum('bhck,bhkv->bhcv', qg_c, S) + jnp.einsum('bhij,bhjv->bhiv', attn_c, v_new)
        S = S * gl_c[..., None, None] + jnp.einsum('bhck,bhcv->bhkv', kt_c, v_new)
        return S, o

    S, o = lax.scan(step, s0, xs)
    o = jnp.transpose(o, (1, 0, 3, 2, 4)).reshape(bsz, n * C, H, dv)[:, :T]
    return o, S


def short_conv_mixer(u, buf, w_in, w_conv, w_out):
    p = u @ w_in
    b_gate = p[..., :D_CONV]
    c_gate = p[..., D_CONV:2 * D_CONV]
    h = p[..., 2 * D_CONV:3 * D_CONV]
    z = p[..., 3 * D_CONV:]
    y, new_buf = causal_dwconv(c_gate * h, buf, w_conv)
    return (b_gate * y * jax.nn.silu(z)) @ w_out, new_buf


def gdn_mixer(u, conv_buf, s0, w_in, w_conv, a_log, dt_bias, norm_w, w_out):
    f32 = jnp.float32
    bsz, T, _ = u.shape
    p = u @ w_in
    qkv, new_buf = causal_dwconv(p[..., :GDN_CONV_CH], conv_buf, w_conv)
    qkv = jax.nn.silu(qkv).astype(f32)
    z = p[..., GDN_CONV_CH:GDN_CONV_CH + GDN_VW]
    b = p[..., GDN_CONV_CH + GDN_VW:GDN_CONV_CH + GDN_VW + GDN_HEADS].astype(f32)
    a = p[..., GDN_CONV_CH + GDN_VW + GDN_HEADS:].astype(f32)
    q = l2norm(qkv[..., :GDN_QK].reshape(bsz, T, GDN_HEADS, GDN_DK))
    k = l2norm(qkv[..., GDN_QK:2 * GDN_QK].reshape(bsz, T, GDN_HEADS, GDN_DK))
    v = qkv[..., 2 * GDN_QK:].reshape(bsz, T, GDN_HEADS, GDN_DV)
    beta = jax.nn.sigmoid(b)
    g = -jnp.exp(a_log.astype(f32)) * jax.nn.softplus(a + dt_bias.astype(f32))
    o, S = gated_delta_rule(q, k, v, g, beta, s0.astype(f32))
    o = o * lax.rsqrt(jnp.mean(o * o, axis=-1, keepdims=True) + NORM_EPS) * norm_w.astype(f32)
    o = o.reshape(bsz, T, GDN_VW).astype(u.dtype) * jax.nn.silu(z)
    return o @ w_out, new_buf, S.astype(s0.dtype)


def trunk(x, c, conv_a, conv_b, ssm_b, w_mod, b_mod, ln_g, ln_b, wa_in, wa_conv, wa_out,
          wb_in, wb_conv, wb_a_log, wb_dt_bias, wb_norm, wb_out):
    new_a, new_cb, new_s = [], [], []
    cs = jax.nn.silu(c)
    for l in range(DEPTH):
        mod = cs @ w_mod[l] + b_mod[l]
        shift = mod[:, None, :D_MODEL]
        scale = mod[:, None, D_MODEL:2 * D_MODEL]
        gate = mod[:, None, 2 * D_MODEL:]
        u = x * (1.0 + scale) + shift
        i = l // N_MIXERS
        if l % N_MIXERS == 0:
            out, nb = short_conv_mixer(u, conv_a[i], wa_in[i], wa_conv[i], wa_out[i])
            new_a.append(nb)
        else:
            out, nb, S = gdn_mixer(u, conv_b[i], ssm_b[i], wb_in[i], wb_conv[i], wb_a_log[i],
                                   wb_dt_bias[i], wb_norm[i], wb_out[i])
            new_cb.append(nb)
            new_s.append(S)
        x = layer_norm(ALPHA * x + gate * out, ln_g[l], ln_b[l])
    return x, jnp.stack(new_a), jnp.stack(new_cb), jnp.stack(new_s)


def setup_inputs(seed: int = 0) -> dict:
    key = jax.random.key(seed)
    ks = jax.random.split(key, 24)
    nrm = lambda k, s, sc: jax.random.normal(k, s, jnp.float32) * sc
    dt = jnp.exp(jax.random.uniform(ks[18], (N_GDN_LAYERS, GDN_HEADS), jnp.float32, np.log(1e-3), np.log(1e-1)))
    return {
        'x_prompt': nrm(ks[0], (BATCH, SEQ, D_MODEL), 1.0),
        'x_sample': nrm(ks[1], (DEC_BATCH, DEC_SEQ, D_MODEL), 1.0),
        'state_conv_a': nrm(ks[2], (N_CONV_LAYERS, DEC_BATCH, CONV_A_WIDTH - 1, D_CONV), 0.5),
        'state_conv_b': nrm(ks[3], (N_GDN_LAYERS, DEC_BATCH, GDN_CONV_WIDTH - 1, GDN_CONV_CH), 1.0),
        'state_ssm_b': nrm(ks[4], (N_GDN_LAYERS, DEC_BATCH, GDN_HEADS, GDN_DK, GDN_DV), 0.1),
        'c_prompt': nrm(ks[5], (BATCH, D_MODEL), 1.0),
        'c_sample': nrm(ks[6], (DEC_BATCH, D_MODEL), 1.0),
        'w_mod': nrm(ks[7], (DEPTH, D_MODEL, 3 * D_MODEL), 0.5 * D_MODEL ** -0.5),
        'b_mod': nrm(ks[8], (DEPTH, 3 * D_MODEL), 0.01),
        'ln_g': 1.0 + nrm(ks[9], (DEPTH, D_MODEL), 0.02),
        'ln_b': nrm(ks[10], (DEPTH, D_MODEL), 0.02),
        'wa_in': nrm(ks[11], (N_CONV_LAYERS, D_MODEL, 4 * D_CONV), D_MODEL ** -0.5),
        'wa_conv': nrm(ks[12], (N_CONV_LAYERS, CONV_A_WIDTH, D_CONV), CONV_A_WIDTH ** -0.5),
        'wa_out': nrm(ks[13], (N_CONV_LAYERS, D_CONV, D_MODEL), BETA_INIT * D_CONV ** -0.5),
        'wb_in': nrm(ks[14], (N_GDN_LAYERS, D_MODEL, GDN_IN), D_MODEL ** -0.5),
        'wb_conv': nrm(ks[15], (N_GDN_LAYERS, GDN_CONV_WIDTH, GDN_CONV_CH), GDN_CONV_WIDTH ** -0.5),
        'wb_a_log': jnp.log(jax.random.uniform(ks[16], (N_GDN_LAYERS, GDN_HEADS), jnp.float32, 1.0, 16.0)),
        'wb_dt_bias': dt + jnp.log(-jnp.expm1(-dt)),
        'wb_norm': 1.0 + nrm(ks[17], (N_GDN_LAYERS, GDN_DV), 0.02),
        'wb_out': nrm(ks[19], (N_GDN_LAYERS, GDN_VW, D_MODEL), BETA_INIT * GDN_VW ** -0.5),
    }


def reference(x_prompt, x_sample, state_conv_a, state_conv_b, state_ssm_b, c_prompt, c_sample,
              w_mod, b_mod, ln_g, ln_b, wa_in, wa_conv, wa_out,
              wb_in, wb_conv, wb_a_log, wb_dt_bias, wb_norm, wb_out):
    bp = x_prompt.shape[0]
    dt_ = x_prompt.dtype
    zero_a = jnp.zeros((N_CONV_LAYERS, bp, CONV_A_WIDTH - 1, D_CONV), dt_)
    zero_cb = jnp.zeros((N_GDN_LAYERS, bp, GDN_CONV_WIDTH - 1, GDN_CONV_CH), dt_)
    zero_s = jnp.zeros((N_GDN_LAYERS, bp, GDN_HEADS, GDN_DK, GDN_DV), state_ssm_b.dtype)
    y_prompt, conv_a_p, conv_b_p, ssm_b_p = trunk(
        x_prompt, c_prompt, zero_a, zero_cb, zero_s, w_mod, b_mod, ln_g, ln_b, wa_in, wa_conv, wa_out,
        wb_in, wb_conv, wb_a_log, wb_dt_bias, wb_norm, wb_out)
    y_sample, conv_a_s, conv_b_s, ssm_b_s = trunk(
        x_sample, c_sample, state_conv_a, state_conv_b, state_ssm_b, w_mod, b_mod, ln_g, ln_b,
        wa_in, wa_conv, wa_out, wb_in, wb_conv, wb_a_log, wb_dt_bias, wb_norm, wb_out)
    return (y_prompt, y_sample, conv_a_p, conv_b_p, ssm_b_p, conv_a_s, conv_b_s, ssm_b_s)
```

```python
import os
import numpy as np
from contextlib import ExitStack
import concourse.bass as bass
import concourse.mybir as mybir
from concourse.bass_utils import run_bass_kernel_spmd

F32 = mybir.dt.float32
F32R = mybir.dt.float32r
BF16 = mybir.dt.bfloat16
AF = mybir.ActivationFunctionType
ALU = mybir.AluOpType

NCORES = 8
D = 1024
NT_P = 2048
NT_S = 128
NTOK = NT_P + NT_S
ALPHA = 4.0 ** 0.25
LN_EPS = 1e-5
NORM_EPS = 1e-6
GDN_IN = 4112


class Buf:
    __slots__ = ("name", "w", "r", "excl")

    def __init__(self, name, excl=False):
        self.name = name
        self.w = None
        self.r = {}
        self.excl = excl


class Eng:
    def __init__(self, name):
        self.name = name
        self.ops = []
        self.count = 0
        self.known = {}
        self.hist = {}


class Sched:
    ENGS = ("tensor", "vector", "scalar", "gpsimd", "sync")

    def __init__(self, nc, stack, n_dma_sems=56, same_engine_sync=("vector", "scalar", "gpsimd")):
        self.nc = nc
        self.eng = {n: Eng(n) for n in self.ENGS}
        self.sems = {}
        for n in self.ENGS:
            self.sems[n] = stack.enter_context(nc.semaphore("s_" + n))
        self.dma_sems = []
        self.dma_pool = {"sync": [], "gpsimd": [], "scalar": []}
        for i in range(n_dma_sems):
            k = "d%d" % i
            self.sems[k] = stack.enter_context(nc.semaphore("s_" + k))
            slot = [k, 0]
            self.dma_sems.append(slot)
            self.dma_pool["gpsimd" if i < 16 else "sync"].append(slot)
        self.dma_pool["scalar"] = self.dma_pool["sync"]
        self.dma_rr = {"sync": 0, "gpsimd": 0, "scalar": 0}
        self.same = set(same_engine_sync)
        self.out_tokens = []
        self.rec = None
        self.trans = os.environ.get("MK_TRANS", "1") == "1"

    def _need(self, e, tokens):
        for k, v in sorted(tokens.items(), key=lambda kv: -kv[1]):
            if k == e.name and e.name not in self.same:
                continue
            if e.known.get(k, 0) >= v:
                continue
            e.known[k] = v
            e.ops.append(("wait", k, v))
            if self.trans and k in self.eng and k != e.name:
                h = self.eng[k].hist.get(v)
                if h:
                    for k2, v2 in h.items():
                        if e.known.get(k2, 0) < v2:
                            e.known[k2] = v2

    @staticmethod
    def _collect(reads, writes):
        tok = {}
        for b in reads:
            if b.w is not None and tok.get(b.w[0], 0) < b.w[1]:
                tok[b.w[0]] = b.w[1]
        for b in writes:
            if b.w is not None and tok.get(b.w[0], 0) < b.w[1]:
                tok[b.w[0]] = b.w[1]
            for k, v in b.r.items():
                if tok.get(k, 0) < v:
                    tok[k] = v
        return tok

    def replay(self, lists):
        self.rec = None
        idx = [0] * len(lists)
        live = True
        while live:
            live = False
            for i, lst in enumerate(lists):
                if idx[i] < len(lst):
                    it = lst[idx[i]]
                    idx[i] += 1
                    live = True
                    if it[0] == "op":
                        self.op(it[1], it[2], it[3], it[4])
                    else:
                        self.dma(it[1], it[2], it[3], it[4], it[5], it[6], **it[7])

    def op(self, eng, fn, reads=(), writes=()):
        if self.rec is not None:
            self.rec.append(("op", eng, fn, list(reads), list(writes)))
            return None
        e = self.eng[eng]
        if eng != "tensor" and any(b.excl for b in reads):
            writes = list(writes) + [b for b in reads if b.excl]
            reads = [b for b in reads if not b.excl]
        self._need(e, self._collect(reads, writes))
        e.count += 1
        e.ops.append(("ins", fn, eng, 1))
        if self.trans:
            e.hist[e.count] = dict(e.known)
        t = (eng, e.count)
        for b in reads:
            if b.r.get(eng, 0) < e.count:
                b.r[eng] = e.count
        for b in writes:
            b.w = t
            b.r = {}
        return t

    def dma(self, q, out, in_, reads=(), writes=(), is_output=False, **kw):
        if self.rec is not None:
            self.rec.append(("dma", q, out, in_, list(reads), list(writes), is_output, kw))
            return None
        e = self.eng[q]
        tok = self._collect(reads, writes)
        pool = self.dma_pool[q]
        slot = pool[self.dma_rr[q] % len(pool)]
        self.dma_rr[q] += 1
        k = slot[0]
        if slot[1] > 0:
            tok[k] = max(tok.get(k, 0), slot[1])
        self._need(e, tok)
        slot[1] += 16
        v = slot[1]
        e.ops.append(("ins", (lambda en, out=out, in_=in_, kw=kw: en.dma_start(out=out, in_=in_, **kw)), k, 16))
        t = (k, v)
        for b in reads:
            b.r[k] = v
        for b in writes:
            b.w = t
            b.r = {}
        if is_output:
            self.out_tokens.append(t)
        return t

    def finish(self, eng="sync"):
        e = self.eng[eng]
        tok = {}
        for k, v in self.out_tokens:
            tok[k] = max(tok.get(k, 0), v)
        for k, v in tok.items():
            e.ops.append(("wait", k, v))

    def emit(self, block):
        sems = self.sems

        def run(en, e):
            for o in e.ops:
                if o[0] == "wait":
                    en.wait_ge(sems[o[1]], o[2])
                else:
                    o[1](en).then_inc(sems[o[2]], o[3])

        @block.tensor
        def _(en):
            run(en, self.eng["tensor"])

        @block.vector
        def _(en):
            run(en, self.eng["vector"])

        @block.scalar
        def _(en):
            run(en, self.eng["scalar"])

        @block.gpsimd
        def _(en):
            run(en, self.eng["gpsimd"])

        @block.sync
        def _(en):
            run(en, self.eng["sync"])

    def stats(self):
        return {n: (sum(1 for o in e.ops if o[0] == "ins"), sum(1 for o in e.ops if o[0] == "wait"))
                for n, e in self.eng.items()}


class Ring:
    def __init__(self, nc, st, name, shape, dt, n):
        self.items = []
        for i in range(n):
            t = st.enter_context(nc.sbuf_tensor("%s%d" % (name, i), shape, dt))
            self.items.append((t, Buf("%s%d" % (name, i))))
        self.i = 0

    def next(self):
        it = self.items[self.i % len(self.items)]
        self.i += 1
        return it


class _Stop(Exception):
    pass


def build(depth_run=2):
    STOP = int(os.environ.get("MK_STOP", "0"))

    def ck(k):
        if STOP == k:
            raise _Stop()
    nc = bass.Bass("TRN2", target_bir_lowering=False)

    def din(name, shape):
        return nc.dram_tensor(name, shape, F32, kind="ExternalInput").ap()

    def dout(name, shape):
        return nc.dram_tensor(name, shape, F32, kind="ExternalOutput").ap()

    xp = din("xp", [NT_P, D])
    xs = din("xs", [NT_S, D])
    cc = din("cc", [17, D])
    sca = din("sca", [32, D])
    scb = din("scb", [48, 3072])
    ssm = din("ssm", [16, 8, 128, 128])
    w_mod = din("w_mod", [2, D, 3072])
    b_mod = din("b_mod", [2, 3072])
    ln_g = din("ln_g", [2, D])
    ln_b = din("ln_b", [2, D])
    wa_in = din("wa_in", [D, 4096])
    wa_conv = din("wa_conv", [3, D])
    wa_out = din("wa_out", [D, D])
    wb_in = din("wb_in", [D, GDN_IN])
    wb_conv = din("wb_conv", [4, 3072])
    wb_a_log = din("wb_a_log", [1, 8])
    wb_dt_bias = din("wb_dt_bias", [1, 8])
    wb_norm = din("wb_norm", [1, 128])
    wb_out = din("wb_out", [D, D])

    yp = dout("yp", [NT_P, D])
    ys = dout("ys", [NT_S, D])
    cap = dout("cap", [2, D])
    cbp = dout("cbp", [3, 3072])
    ssp = dout("ssp", [8, 128, 128])
    cas = dout("cas", [32, D])
    cbs = dout("cbs", [48, 3072])
    sss = dout("sss", [16, 8, 128, 128])
    DBG = int(os.environ.get("MK_DBG", "0"))
    dbg_t = dout("dbg", [24, 128, 128]) if DBG else None
    dbg_names = []
    build.dbg_names = dbg_names

    groups = [("p", g * 512, 512, g * 512) for g in range(4)] + [("s", 0, 128, NT_P)]

    with ExitStack() as st:
        def sb(name, shape, dt=F32):
            return st.enter_context(nc.sbuf_tensor(name, shape, dt))

        XR = sb("XR", [128, 8, NTOK], BF16)
        XG = sb("XG", [128, 8, 512])
        xg_b = Buf("xg")
        xr_b = [Buf("xr%d" % i) for i in range(len(groups))]
        WIN = sb("WIN", [128, 8, GDN_IN], BF16)
        win_b = [Buf("win%d" % j) for j in range(9)]
        WOUT = sb("WOUT", [128, 8, D], BF16)
        wout_b = Buf("wout")
        UT = sb("UT", [128, 8, 512], BF16)
        ut_b = Buf("ut")
        YT = sb("YT", [128, 8, 512], BF16)
        yt_b = Buf("yt")
        ident = sb("ident", [128, 128])
        ident_b = Buf("ident")
        ones_bf = sb("ones_bf", [128, 128], BF16)
        ones_b = Buf("ones")
        PA_in = sb("PA_in", [104, 128])
        PB_in = sb("PB_in", [96, 128])
        PA = sb("PA", [128, 104])
        PB = sb("PB", [128, 96])
        pa_in_b = Buf("pa_in"); pb_in_b = Buf("pb_in"); pa_b = Buf("pa"); pb_b = Buf("pb")
        csT = sb("csT", [128, 8, 17], BF16)
        cst_b = Buf("csT")
        modT = [sb("modT%d" % l, [128, 24, 17]) for l in range(2)]
        modT_b = [Buf("modT%d" % l) for l in range(2)]
        xt_ring = Ring(nc, st, "xt", [128, D], F32, 2)
        ctile, ct_b = xt_ring.next()
        w512 = Ring(nc, st, "w512", [128, 512], F32, 4)
        stat = Ring(nc, st, "stat", [128, 512], F32, 4)
        hist_a = sb("hist_a", [128, 8, 2])
        hist_a_b = Buf("hist_a")
        chs = sb("chs", [128, 8, 16, 10])
        chs_b = [Buf("chs%d" % i) for i in range(8)]
        chp = Ring(nc, st, "chp", [128, 514], F32, 2)
        hist_s = sb("hist_s", [128, 8, 32])
        hist_s_b = Buf("hist_s")
        otile = xt_ring
        small_o = sb("small_o", [48, 1024])
        small_o_b = Buf("small_o")
        mod_tm, modtm_b = XG[:].rearrange("p c t -> p (c t)"), xg_b

        ps = [st.enter_context(nc.psum_tensor("ps%d" % i, [128, 512], F32)) for i in range(8)]
        ps_b = [Buf("ps%d" % i, excl=True) for i in range(8)]
        block = st.enter_context(nc.Block())
        _same = tuple(x for x in os.environ.get("MK_SAME", "vector,scalar,gpsimd").split(",") if x)
        S = Sched(nc, st, same_engine_sync=_same)

        psrr = [0]

        def dbg(name, ap, buf, ncols=128):
            if not DBG or len(dbg_names) >= 24:
                return
            i = len(dbg_names)
            dbg_names.append(name)
            S.dma("gpsimd", dbg_t[i][:, 0:ncols], ap, reads=[buf], is_output=True)

        def next_ps(lo=0, hi=4):
            i = lo + psrr[0] % (hi - lo)
            psrr[0] += 1
            return ps[i], ps_b[i]

        S.op("gpsimd", lambda e: e.memset(ident[:], 1.0), writes=[ident_b])
        S.op("gpsimd", lambda e: e.affine_select(out=ident[:], in_=ident[:], pattern=[[-1, 128]], compare_op=ALU.is_equal,
                                                 fill=0.0, base=0, channel_multiplier=1), reads=[ident_b], writes=[ident_b])
        S.op("gpsimd", lambda e: e.memset(ones_bf[:], 1.0), writes=[ones_b])
        S.op("gpsimd", lambda e: e.memset(hist_a[:], 0.0), writes=[hist_a_b])

        def rows(dst, r0, src, nrows):
            S.dma("sync", dst[r0:r0 + nrows, :], src, writes=[pa_in_b if dst is PA_in else pb_in_b])
        for l in range(2):
            rows(PA_in, l * 8, ln_g[l].rearrange("(c p) -> c p", p=128), 8)
            rows(PA_in, 16 + l * 8, ln_b[l].rearrange("(c p) -> c p", p=128), 8)
            rows(PA_in, 56 + l * 24, b_mod[l].rearrange("(c p) -> c p", p=128), 24)
        for j in range(3):
            rows(PA_in, 32 + j * 8, wa_conv[j].rearrange("(c p) -> c p", p=128), 8)
        for j in range(4):
            rows(PB_in, j * 24, wb_conv[j].rearrange("(c p) -> c p", p=128), 24)
        S.dma("sync", ctile[0:17, :], cc, writes=[ct_b])
        pt, ptb = ps[4], ps_b[4]
        S.op("tensor", lambda e: e.transpose(pt[:, 0:104], PA_in[:, :], ident[0:104, 0:104]), reads=[pa_in_b, ident_b], writes=[ptb])
        S.op("vector", lambda e: e.tensor_copy(out=PA[:], in_=pt[:, 0:104]), reads=[ptb], writes=[pa_b])
        S.op("tensor", lambda e: e.transpose(pt[:, 128:224], PB_in[:, :], ident[0:96, 0:96]), reads=[pb_in_b, ident_b], writes=[ptb])
        S.op("vector", lambda e: e.tensor_copy(out=PB[:], in_=pt[:, 128:224]), reads=[ptb], writes=[pb_b])

        def lng(l, c):
            return PA[:, l * 8 + c:l * 8 + c + 1]

        def lnb(l, c):
            return PA[:, 16 + l * 8 + c:16 + l * 8 + c + 1]

        def waconv(j, c):
            return PA[:, 32 + j * 8 + c:32 + j * 8 + c + 1]

        def wbconv(j, c):
            return PB[:, j * 24 + c:j * 24 + c + 1]

        S.op("scalar", lambda e: e.activation(out=ctile[0:17, :], in_=ctile[0:17, :], func=AF.Silu), reads=[ct_b], writes=[ct_b])
        for c in range(8):
            S.op("tensor", lambda e, c=c: e.transpose(pt[:, 256 + c * 17:256 + (c + 1) * 17], ctile[0:17, c * 128:(c + 1) * 128], ident[0:17, 0:17]),
                 reads=[ct_b, ident_b], writes=[ptb])
        S.op("vector", lambda e: e.tensor_copy(out=csT[:], in_=pt[:, 256:256 + 136].rearrange("p (c s) -> p c s", s=17)),
             reads=[ptb], writes=[cst_b])

        wm_views = [(UT, ut_b), (YT, yt_b)]
        for l in range(depth_run):
            for j in range(6):
                wm, wmb = wm_views[j % 2]
                S.dma("gpsimd", wm[:, :, :], w_mod[l][:, j * 512:(j + 1) * 512].rearrange("(c p) n -> p c n", p=128), writes=[wmb])
                pm, pmb = next_ps(0, 4)
                for k in range(8):
                    S.op("tensor", lambda e, k=k, wm=wm, pm=pm: e.matmul(pm[0:17, :], lhsT=csT[:, k, :], rhs=wm[:, k, :], start=(k == 0), stop=(k == 7)),
                         reads=[cst_b, wmb], writes=[pmb])
                S.op("scalar", lambda e, j=j, pm=pm: e.copy(out=mod_tm[0:17, j * 512:(j + 1) * 512], in_=pm[0:17, :]), reads=[pmb], writes=[modtm_b])
            pm, pmb = ps[5], ps_b[5]
            for ch in range(24):
                S.op("tensor", lambda e, ch=ch: e.transpose(pm[:, ch * 17:(ch + 1) * 17], mod_tm[0:17, ch * 128:(ch + 1) * 128], ident[0:17, 0:17]),
                     reads=[modtm_b, ident_b], writes=[pmb])
            S.op("vector", lambda e, l=l: e.tensor_tensor(out=modT[l][:], in0=pm[:, 0:408].rearrange("p (c s) -> p c s", s=17),
                                                       in1=PA[:, 56 + l * 24:56 + l * 24 + 24].unsqueeze(2).to_broadcast([128, 24, 17]), op=ALU.add),
                 reads=[pmb, pa_b], writes=[modT_b[l]])
            S.op("vector", lambda e, l=l: e.tensor_scalar_add(out=modT[l][:, 8:16, :], in0=modT[l][:, 8:16, :], scalar1=1.0),
                 reads=[modT_b[l]], writes=[modT_b[l]])
            S.op("vector", lambda e, l=l: e.tensor_scalar_mul(out=modT[l][:, 16:24, :], in0=modT[l][:, 16:24, :], scalar1=1.0 / ALPHA),
                 reads=[modT_b[l]], writes=[modT_b[l]])

        ck(1)
        def load_win(src, ncols):
            v = src.rearrange("(c p) n -> p c n", p=128)
            nb = (ncols + 511) // 512
            for j in range(nb):
                a, b = j * 512, min(ncols, (j + 1) * 512)
                S.dma("gpsimd", WIN[:, :, a:b], v[:, :, a:b], writes=[win_b[j]])

        def load_wout(src):
            S.dma("gpsimd", WOUT[:, :, :], src.rearrange("(c p) n -> p c n", p=128), writes=[wout_b])

        def load_group(gi):
            kind, t0, n, col0 = groups[gi]
            src = xp if kind == "p" else xs
            for t in range(n // 128):
                xt, xtb = xt_ring.next()
                S.dma("sync", xt[:], src[t0 + t * 128:t0 + (t + 1) * 128, :], writes=[xtb])
                for half in range(2):
                    pp, ppb = next_ps(4, 8)
                    for q in range(4):
                        c = half * 4 + q
                        S.op("tensor", lambda e, c=c, q=q, pp=pp, xt=xt: e.transpose(pp[:, q * 128:(q + 1) * 128], xt[:, c * 128:(c + 1) * 128], ident[:]),
                             reads=[xtb, ident_b], writes=[ppb])
                    dst = XG[:, half * 4:half * 4 + 4, t * 128:(t + 1) * 128]
                    S.op("scalar" if half == 0 else "vector",
                         (lambda e, dst=dst, pp=pp: e.copy(out=dst, in_=pp[:, :].rearrange("p (c t) -> p c t", t=128))) if half == 0 else
                         (lambda e, dst=dst, pp=pp: e.tensor_copy(out=dst, in_=pp[:, :].rearrange("p (c t) -> p c t", t=128))),
                         reads=[ppb], writes=[xg_b])

        def make_u(l, gi):
            kind, t0, n, col0 = groups[gi]
            SRC, srcb, so = (XG, xg_b, 0) if l == 0 else (XR, xr_b[gi], col0)
            for c in range(8):
                if kind == "p":
                    S.op("scalar", lambda e, c=c: e.activation(out=UT[:, c, 0:n], in_=SRC[:, c, so:so + n], func=AF.Identity,
                                                               scale=modT[l][:, 8 + c, 0:1], bias=modT[l][:, c, 0:1]),
                         reads=[srcb, modT_b[l]], writes=[ut_b])
                else:
                    tmp, tmpb = w512.next()
                    S.op("vector", lambda e, c=c, tmp=tmp: e.tensor_tensor(out=tmp[:, 0:128].rearrange("p (s t) -> p s t", t=8),
                                                                           in0=SRC[:, c, so:so + 128].rearrange("p (s t) -> p s t", t=8),
                                                                           in1=modT[l][:, 8 + c, 1:17].unsqueeze(2).to_broadcast([128, 16, 8]), op=ALU.mult),
                         reads=[srcb, modT_b[l]], writes=[tmpb])
                    S.op("vector", lambda e, c=c, tmp=tmp: e.tensor_tensor(out=UT[:, c, 0:128].rearrange("p (s t) -> p s t", t=8),
                                                                           in0=tmp[:, 0:128].rearrange("p (s t) -> p s t", t=8),
                                                                           in1=modT[l][:, c, 1:17].unsqueeze(2).to_broadcast([128, 16, 8]), op=ALU.add),
                         reads=[tmpb, modT_b[l]], writes=[ut_b])

        def inproj(m, n, pm, pmb, mw=128):
            for k in range(8):
                S.op("tensor", lambda e, k=k: e.matmul(pm[0:mw, 0:n], lhsT=WIN[:, k, m * 128:m * 128 + mw], rhs=UT[:, k, 0:n], start=(k == 0), stop=(k == 7)),
                     reads=[win_b[(m * 128) // 512], ut_b], writes=[pmb])

        def outproj_ln(l, gi, last):
            kind, t0, n, col0 = groups[gi]
            eps2 = LN_EPS / (ALPHA * ALPHA)
            for m in range(8):
                pm, pmb = next_ps(0, 4)
                for k in range(8):
                    S.op("tensor", lambda e, k=k, m=m, pm=pm: e.matmul(pm[:, 0:n], lhsT=WOUT[:, k, m * 128:(m + 1) * 128], rhs=YT[:, k, 0:n], start=(k == 0), stop=(k == 7)),
                         reads=[wout_b, yt_b], writes=[pmb])
                xv = XG[:, m, 0:n]
                if l == 0:
                    res, resb = xv, xg_b
                else:
                    res, resb = XR[:, m, col0:col0 + n], xr_b[gi]
                if kind == "p":
                    S.op("vector", lambda e, m=m, pm=pm, xv=xv, res=res: e.scalar_tensor_tensor(out=xv, in0=pm[:, 0:n], scalar=modT[l][:, 16 + m, 0:1], in1=res,
                                                                                                  op0=ALU.mult, op1=ALU.add),
                         reads=[pmb, modT_b[l], resb, xg_b], writes=[xg_b])
                else:
                    tmp, tmpb = w512.next()
                    S.op("vector", lambda e, m=m, pm=pm, tmp=tmp: e.tensor_tensor(out=tmp[:, 0:128].rearrange("p (s t) -> p s t", t=8),
                                                                                   in0=pm[:, 0:128].rearrange("p (s t) -> p s t", t=8),
                                                                                   in1=modT[l][:, 16 + m, 1:17].unsqueeze(2).to_broadcast([128, 16, 8]), op=ALU.mult),
                         reads=[pmb, modT_b[l]], writes=[tmpb])
                    S.op("gpsimd", lambda e, tmp=tmp, xv=xv, res=res: e.tensor_tensor(out=xv, in0=res, in1=tmp[:, 0:128], op=ALU.add),
                         reads=[tmpb, resb, xg_b], writes=[xg_b])
            for c in range(8):
                xv = XG[:, c, 0:n]
                S.op("vector", lambda e, c=c, xv=xv: e.tensor_copy(out=UT[:, c, 0:n], in_=xv), reads=[xg_b], writes=[ut_b])
                S.op("scalar", lambda e, c=c, xv=xv: e.activation(out=YT[:, c, 0:n], in_=xv, func=AF.Square), reads=[xg_b], writes=[yt_b])
            p1, p1b = next_ps(0, 4)
            p2, p2b = next_ps(0, 4)
            for c in range(8):
                S.op("tensor", lambda e, c=c, p1=p1: e.matmul(p1[:, 0:n], lhsT=ones_bf[:], rhs=UT[:, c, 0:n], start=(c == 0), stop=(c == 7)),
                     reads=[ones_b, ut_b], writes=[p1b])
            for c in range(8):
                S.op("tensor", lambda e, c=c, p2=p2: e.matmul(p2[:, 0:n], lhsT=ones_bf[:], rhs=YT[:, c, 0:n], start=(c == 0), stop=(c == 7)),
                     reads=[ones_b, yt_b], writes=[p2b])
            mean, meanb = stat.next()
            msq, msqb = stat.next()
            rstd, rstdb = stat.next()
            S.op("scalar", lambda e: e.mul(out=mean[:, 0:n], in_=p1[:, 0:n], mul=1.0 / D), reads=[p1b], writes=[meanb])
            S.op("vector", lambda e: e.tensor_tensor(out=msq[:, 0:n], in0=mean[:, 0:n], in1=mean[:, 0:n], op=ALU.mult), reads=[meanb], writes=[msqb])
            S.op("vector", lambda e: e.scalar_tensor_tensor(out=rstd[:, 0:n], in0=p2[:, 0:n], scalar=1.0 / D, in1=msq[:, 0:n], op0=ALU.mult, op1=ALU.subtract),
                 reads=[p2b, msqb], writes=[rstdb])
            S.op("vector", lambda e: e.tensor_scalar(out=rstd[:, 0:n], in0=rstd[:, 0:n], scalar1=0.0, scalar2=eps2, op0=ALU.max, op1=ALU.add),
                 reads=[rstdb], writes=[rstdb])
            S.op("scalar", lambda e: e.activation(out=rstd[:, 0:n], in_=rstd[:, 0:n], func=AF.Ln), reads=[rstdb], writes=[rstdb])
            S.op("scalar", lambda e: e.activation(out=rstd[:, 0:n], in_=rstd[:, 0:n], func=AF.Exp, scale=-0.5), reads=[rstdb], writes=[rstdb])
            for c in range(8):
                xv = XG[:, c, 0:n]
                eng = "vector" if c % 2 == 0 else "gpsimd"
                S.op(eng, lambda e, xv=xv: e.tensor_tensor(out=xv, in0=xv, in1=mean[:, 0:n], op=ALU.subtract), reads=[xg_b, meanb], writes=[xg_b])
                S.op(eng, lambda e, xv=xv: e.tensor_tensor(out=xv, in0=xv, in1=rstd[:, 0:n], op=ALU.mult), reads=[xg_b, rstdb], writes=[xg_b])
                if last:
                    S.op("scalar", lambda e, xv=xv, c=c: e.activation(out=xv, in_=xv, func=AF.Identity, scale=lng(l, c), bias=lnb(l, c)),
                         reads=[xg_b, pa_b], writes=[xg_b])
                else:
                    S.op("scalar", lambda e, xv=xv, c=c: e.activation(out=XR[:, c, col0:col0 + n], in_=xv, func=AF.Identity, scale=lng(l, c), bias=lnb(l, c)),
                         reads=[xg_b, pa_b], writes=[xr_b[gi]])
            if last:
                store_group(gi)

        def store_group(gi):
            kind, t0, n, col0 = groups[gi]
            dst = yp if kind == "p" else ys
            for t in range(n // 128):
                ot, otb = otile.next()
                for half in range(2):
                    pp, ppb = next_ps(4, 8)
                    for q in range(4):
                        c = half * 4 + q
                        S.op("tensor", lambda e, c=c, q=q, pp=pp, t=t: e.transpose(pp[:, q * 128:(q + 1) * 128], XG[:, c, t * 128:(t + 1) * 128], ident[:]),
                             reads=[xg_b, ident_b], writes=[ppb])
                    if half == 0:
                        S.op("scalar", lambda e, pp=pp, ot=ot: e.copy(out=ot[:, 0:512], in_=pp[:, :]), reads=[ppb], writes=[otb])
                    else:
                        S.op("vector", lambda e, pp=pp, ot=ot: e.tensor_copy(out=ot[:, 512:1024], in_=pp[:, :]), reads=[ppb], writes=[otb])
                S.dma("sync", dst[t0 + t * 128:t0 + (t + 1) * 128, :], ot[:], reads=[otb], is_output=True)

        def layer0():
            l = 0
            load_win(wa_in, 4096)
            load_wout(wa_out)
            sct, sctb = xt_ring.next()
            S.dma("sync", sct[0:32, :], sca, writes=[sctb])
            pp, ppb = next_ps(4, 8)
            for c in range(8):
                S.op("tensor", lambda e, c=c: e.transpose(pp[:, c * 32:(c + 1) * 32], sct[0:32, c * 128:(c + 1) * 128], ident[0:32, 0:32]),
                     reads=[sctb, ident_b], writes=[ppb])
            S.op("vector", lambda e: e.tensor_copy(out=chs[:, :, :, 0:2], in_=pp[:, 0:256].rearrange("p (c s j) -> p c s j", s=16, j=2)),
                 reads=[ppb], writes=chs_b)
            for gi in range(len(groups)):
                l0_group(gi)

        def l0_group(gi):
            if True:
                l = 0
                kind, t0, n, col0 = groups[gi]
                load_group(gi)
                ck(2)
                if gi == 4:
                    ck(30)
                make_u(l, gi)
                ck(3)
                if gi == 4:
                    ck(31)
                for i in range(8):
                    pc, pcb = next_ps(0, 4)
                    inproj(8 + i, n, pc, pcb)
                    csb, csbb = w512.next()
                    S.op("scalar", lambda e, pc=pc, csb=csb: e.copy(out=csb[:, 0:n], in_=pc[:, 0:n]), reads=[pcb], writes=[csbb])
                    ph, phb = next_ps(0, 4)
                    inproj(16 + i, n, ph, phb)
                    ysb, ysbb = w512.next()
                    if kind == "p":
                        ch, chb = chp.next()
                        S.op("gpsimd", lambda e, ch=ch, i=i: e.tensor_copy(out=ch[:, 0:2], in_=hist_a[:, i, :]), reads=[hist_a_b], writes=[chb])
                        S.op("vector", lambda e, ch=ch, csb=csb, ph=ph: e.tensor_tensor(out=ch[:, 2:2 + n], in0=csb[:, 0:n], in1=ph[:, 0:n], op=ALU.mult),
                             reads=[csbb, phb], writes=[chb])
                        S.op("gpsimd", lambda e, ch=ch, i=i: e.tensor_copy(out=hist_a[:, i, :], in_=ch[:, n:n + 2]), reads=[chb], writes=[hist_a_b])
                        taps = [ch[:, j:j + n] for j in range(3)]
                        yv = ysb[:, 0:n]
                    else:
                        chb = chs_b[i]
                        S.op("vector", lambda e, i=i, csb=csb, ph=ph: e.tensor_tensor(out=chs[:, i, :, 2:10], in0=csb[:, 0:128].rearrange("p (s t) -> p s t", t=8),
                                                                                       in1=ph[:, 0:128].rearrange("p (s t) -> p s t", t=8), op=ALU.mult),
                             reads=[csbb, phb], writes=[chb])
                        taps = [chs[:, i, :, j:j + 8] for j in range(3)]
                        yv = ysb[:, 0:128].rearrange("p (s t) -> p s t", t=8)
                    S.op("scalar", lambda e, yv=yv, taps=taps, i=i: e.activation(out=yv, in_=taps[0], func=AF.Identity, scale=waconv(0, i)),
                         reads=[chb, pa_b], writes=[ysbb])
                    S.op("vector", lambda e, yv=yv, taps=taps, i=i: e.scalar_tensor_tensor(out=yv, in0=taps[1], scalar=waconv(1, i), in1=yv, op0=ALU.mult, op1=ALU.add),
                         reads=[chb, pa_b, ysbb], writes=[ysbb])
                    S.op("vector", lambda e, yv=yv, taps=taps, i=i: e.scalar_tensor_tensor(out=yv, in0=taps[2], scalar=waconv(2, i), in1=yv, op0=ALU.mult, op1=ALU.add),
                         reads=[chb, pa_b, ysbb], writes=[ysbb])
                    pz, pzb = next_ps(0, 4)
                    inproj(24 + i, n, pz, pzb)
                    szb, szbb = w512.next()
                    S.op("scalar", lambda e, pz=pz, szb=szb: e.activation(out=szb[:, 0:n], in_=pz[:, 0:n], func=AF.Silu), reads=[pzb], writes=[szbb])
                    pb_, pbb = next_ps(0, 4)
                    inproj(i, n, pb_, pbb)
                    S.op("vector", lambda e, pb_=pb_, szb=szb: e.tensor_tensor(out=szb[:, 0:n], in0=pb_[:, 0:n], in1=szb[:, 0:n], op=ALU.mult),
                         reads=[pbb, szbb], writes=[szbb])
                    S.op("vector", lambda e, i=i, ysb=ysb, szb=szb: e.tensor_tensor(out=YT[:, i, 0:n], in0=ysb[:, 0:n], in1=szb[:, 0:n], op=ALU.mult),
                         reads=[ysbb, szbb], writes=[yt_b])
                ck(4)
                if gi == 4:
                    ck(32)
                outproj_ln(l, gi, depth_run == 1)
                ck(5)
                ck(11 + gi) if gi >= 1 else None
        def layer0_tail():
            ck(20)
            pp, ppb = next_ps(4, 8)
            for c in range(4):
                S.op("tensor", lambda e, c=c: e.transpose(pp[0:2, c * 128:(c + 1) * 128], hist_a[:, c, :], ident[:]),
                     reads=[hist_a_b, ident_b], writes=[ppb])
            S.op("vector", lambda e: e.tensor_copy(out=hist_s[:].rearrange("p c (s j) -> p c s j", j=2), in_=chs[:, :, :, 8:10]),
                 reads=chs_b, writes=[hist_s_b])
            pp2, pp2b = next_ps(4, 8)
            for c in range(4):
                S.op("tensor", lambda e, c=c: e.transpose(pp2[0:32, c * 128:(c + 1) * 128], hist_s[:, c, :], ident[:]),
                     reads=[hist_s_b, ident_b], writes=[pp2b])
            pp3, pp3b = next_ps(4, 8)
            for c in range(4, 8):
                S.op("tensor", lambda e, c=c: e.transpose(pp3[0:32, (c - 4) * 128:(c - 3) * 128], hist_s[:, c, :], ident[:]),
                     reads=[hist_s_b, ident_b], writes=[pp3b])
            S.op("vector", lambda e: e.tensor_copy(out=small_o[0:32, 0:512], in_=pp2[0:32, :]), reads=[pp2b], writes=[small_o_b])
            S.op("vector", lambda e: e.tensor_copy(out=small_o[0:32, 512:1024], in_=pp3[0:32, :]), reads=[pp3b], writes=[small_o_b])
            S.dma("sync", cas, small_o[0:32, 0:1024], reads=[small_o_b], is_output=True)
            S.op("vector", lambda e: e.tensor_copy(out=small_o[0:2, 0:512], in_=pp[0:2, :]), reads=[ppb], writes=[small_o_b])
            S.op("tensor", lambda e: e.transpose(pp[0:2, 0:128], hist_a[:, 4, :], ident[:]), reads=[hist_a_b, ident_b], writes=[ppb])
            S.op("tensor", lambda e: e.transpose(pp[0:2, 128:256], hist_a[:, 5, :], ident[:]), reads=[hist_a_b, ident_b], writes=[ppb])
            S.op("tensor", lambda e: e.transpose(pp[0:2, 256:384], hist_a[:, 6, :], ident[:]), reads=[hist_a_b, ident_b], writes=[ppb])
            S.op("tensor", lambda e: e.transpose(pp[0:2, 384:512], hist_a[:, 7, :], ident[:]), reads=[hist_a_b, ident_b], writes=[ppb])
            S.op("vector", lambda e: e.tensor_copy(out=small_o[0:2, 512:1024], in_=pp[0:2, :]), reads=[ppb], writes=[small_o_b])
            S.dma("sync", cap, small_o[0:2, 0:1024], reads=[small_o_b], is_output=True)

        INV_DT = BF16 if os.environ.get("MK_INV", "f32") == "bf16" else F32

        def barrier():
            toks = {}
            for n_, e_ in S.eng.items():
                if e_.count:
                    toks[n_] = e_.count
            for k_, v_ in S.dma_sems:
                if v_:
                    toks[k_] = v_
            for n_, e_ in S.eng.items():
                S._need(e_, dict(toks))

        def layer1():
            l = 1
            barrier()
            base = len(groups)
            for ti in range(16):
                groups.append(("p", ti * 128, 128, ti * 128))
                xr_b.append(xr_b[ti // 4])
            groups.append(("s", 0, 128, NT_P))
            xr_b.append(xr_b[4])
            load_win(wb_in, GDN_IN)
            load_wout(wb_out)
            f32_tmps = []
            for (t_, _b) in w512.items:
                for q in range(4):
                    f32_tmps.append(t_[:, q * 128:(q + 1) * 128])
            for c in range(8):
                for q in range(1, 4):
                    f32_tmps.append(XG[:, c, q * 128:(q + 1) * 128])
            bf_tmps = []
            for c in range(8):
                for q in range(1, 4):
                    bf_tmps.append(UT[:, c, q * 128:(q + 1) * 128])
                    bf_tmps.append(YT[:, c, q * 128:(q + 1) * 128])
            for (t_, _b) in stat.items[:3]:
                for q in range(1, 4):
                    f32_tmps.append(t_[:, q * 128:(q + 1) * 128])
            res_f = [f32_tmps.pop() for _ in range(15)]
            SIN = sb("SIN", [128, 16, 128]); sinb = Buf("SIN")
            NPAR = int(os.environ.get("MK_NPAR", "3"))
            pools_f = [[f32_tmps.pop() for _ in range(14)], [SIN[:, s_, :] for s_ in range(14)]]
            INVT = sb("INVT", [128, 12, 128])
            pools_i = [[INVT[:, p_ * 6 + i, :] for i in range(6)] for p_ in range(2)]
            wm_b = [bf_tmps.pop() for _ in range(4)]
            pools_b = [[bf_tmps.pop() for _ in range(18)], [bf_tmps.pop() for _ in range(18)]]
            if NPAR >= 3:
                pools_f.append([f32_tmps.pop() for _ in range(14)])
                pools_i.append([f32_tmps.pop() for _ in range(6)])
                xt0f = xt_ring.items[0][0]
                xt0 = xt0f[:, 0:576].bitcast(BF16)
                pools_b.append([xt0[:, i * 128:(i + 1) * 128] for i in range(9)] + [bf_tmps.pop() for _ in range(8)])
                pre_extra_t = [xt0f[:, 576 + i * 131:576 + (i + 1) * 131] for i in range(3)]
            cnt = {"w": 0}
            for p_ in range(3):
                cnt["f%d" % p_] = 0; cnt["b%d" % p_] = 0; cnt["i%d" % p_] = 0
            bufs_f = [[Buf("tf%d_%d" % (p_, i)) for i in range(14)] for p_ in range(3)]
            bufs_i = [[Buf("ti%d_%d" % (p_, i)) for i in range(6)] for p_ in range(3)]
            bufs_b = [[Buf("tb%d_%d" % (p_, i)) for i in range(18)] for p_ in range(3)]
            wm_bufs = [Buf("wm%d" % i) for i in range(4)]
            assert INV_DT == F32

            def T(par=0):
                k_ = "f%d" % par
                i = cnt[k_] % 14
                cnt[k_] += 1
                return pools_f[par][i], bufs_f[par][i]

            def TB(par=0):
                k_ = "b%d" % par
                i = cnt[k_] % len(pools_b[par])
                cnt[k_] += 1
                return pools_b[par][i], bufs_b[par][i]

            def TI(par=0):
                k_ = "i%d" % par
                i = cnt[k_] % 6
                cnt[k_] += 1
                return pools_i[par][i], bufs_i[par][i]

            def WM():
                i = cnt["w"] % 4
                cnt["w"] += 1
                return wm_b[i], wm_bufs[i]
            ones_f = res_f[0]; ones_fb = Buf("ones_f")
            S.op("gpsimd", lambda e: e.memset(ones_f[:], 1.0), writes=[ones_fb])
            masks = {}
            mb = Buf("masks")
            for nm, pat, base_, cm in (("MUI", [[1, 128]], 0, -1), ("MLS", [[-1, 128]], -1, 1), ("MUS", [[1, 128]], -1, -1)):
                for kd in ("p", "s"):
                    masks[(nm, kd)] = res_f[1 + len(masks)]
                S.op("gpsimd", lambda e, nm=nm, pat=pat, base_=base_, cm=cm: e.affine_select(out=masks[(nm, "p")][:], in_=ones_f[:], pattern=pat, compare_op=ALU.is_ge,
                                                                                           fill=0.0, base=base_, channel_multiplier=cm), reads=[ones_fb], writes=[mb])
            SS = sb("SS", [128, 128])
            S.op("gpsimd", lambda e: e.affine_select(out=SS[:].rearrange("p (s t) -> p s t", t=8), in_=ones_f[:].rearrange("p (s t) -> p s t", t=8), pattern=[[-8, 16], [0, 8]],
                                                     compare_op=ALU.is_ge, fill=0.0, base=0, channel_multiplier=1), reads=[ones_fb], writes=[mb])
            S.op("gpsimd", lambda e: e.affine_select(out=SS[:].rearrange("p (s t) -> p s t", t=8), in_=SS[:].rearrange("p (s t) -> p s t", t=8), pattern=[[8, 16], [0, 8]],
                                                     compare_op=ALU.is_ge, fill=0.0, base=7, channel_multiplier=-1), reads=[mb], writes=[mb])
            for nm in ("MUI", "MLS", "MUS"):
                S.op("gpsimd", lambda e, nm=nm: e.tensor_tensor(out=masks[(nm, "s")][:], in0=masks[(nm, "p")][:], in1=SS[:], op=ALU.mult), reads=[mb], writes=[mb])
            rowmask = sb("rowmask", [128, 16])
            S.op("gpsimd", lambda e: e.affine_select(out=rowmask[:], in_=ones_f[:, 0:16], pattern=[[-8, 16]], compare_op=ALU.is_ge, fill=0.0, base=0, channel_multiplier=1),
                 reads=[ones_fb], writes=[mb])
            S.op("gpsimd", lambda e: e.affine_select(out=rowmask[:], in_=rowmask[:], pattern=[[8, 16]], compare_op=ALU.is_ge, fill=0.0, base=7, channel_multiplier=-1),
                 reads=[mb], writes=[mb])
            ident_bf = sb("ident_bf", [128, 128], BF16)
            S.op("vector", lambda e: e.tensor_copy(out=ident_bf[:], in_=ident[:]), reads=[ident_b], writes=[mb])
            prm = sb("prm", [128, 160]); prmb = Buf("prm")
            S.dma("sync", prm[:, 0:8], wb_dt_bias[0, :].partition_broadcast(128), writes=[prmb])
            S.dma("sync", prm[:, 8:16], wb_a_log[0, :].partition_broadcast(128), writes=[prmb])
            S.dma("sync", prm[:, 32:160], wb_norm[0, :].partition_broadcast(128), writes=[prmb])
            S.op("scalar", lambda e: e.activation(out=prm[:, 16:24], in_=prm[:, 8:16], func=AF.Exp), reads=[prmb], writes=[prmb])
            S.op("vector", lambda e: e.tensor_scalar_mul(out=prm[:, 16:24], in0=prm[:, 16:24], scalar1=-1.0), reads=[prmb], writes=[prmb])
            HIST = sb("HIST", [128, 24, 3]); histb = [Buf("HIST%d" % c) for c in range(24)]
            S.op("gpsimd", lambda e: e.memset(HIST[:], 0.0), writes=histb)
            pre_bufs = {"p": [], "s": []}
            for (t_, _b) in chp.items:
                for q in range(3):
                    pre_bufs["p"].append((t_[:, q * 171:(q + 1) * 171], Buf("prep")))
                for q in range(2):
                    pre_bufs["s"].append((t_[:, q * 257:(q + 1) * 257], Buf("pres")))
            pre_rr = [0]

            pre_rr2 = [0, 0, 0]
            pre_extra = [(pre_extra_t[i], Buf("prex%d" % i)) for i in range(3)] if NPAR >= 3 else []
            ipb_rr = [0, 0, 0]

            def PRE(kind_, par_=0):
                lst = pre_bufs[kind_]
                if kind_ == "s":
                    it = lst[pre_rr[0] % len(lst)]
                    pre_rr[0] += 1
                    return it
                if par_ == 2:
                    it = pre_extra[pre_rr2[2] % 3]
                    pre_rr2[2] += 1
                    return it
                it = lst[par_ * 3 + pre_rr2[par_] % 3]
                pre_rr2[par_] += 1
                return it

            def ipbank(par_):
                if NPAR >= 3:
                    return ps[par_], ps_b[par_]
                i = par_ * 2 + ipb_rr[par_] % 2
                ipb_rr[par_] += 1
                return ps[i], ps_b[i]

            def miscbank(par_):
                if NPAR >= 3:
                    return ps[3 + par_], ps_b[3 + par_]
                return ps[4 + par_], ps_b[4 + par_]

            def chainbank(par_):
                if NPAR >= 3:
                    return ps[3 + par_], ps_b[3 + par_]
                return ps[6 + par_], ps_b[6 + par_]
            Sst = [res_f[7 + h] for h in range(8)]; sstb = [Buf("Sst%d" % h) for h in range(8)]
            stat3 = stat.items.pop()[0]
            Sbf = stat3[:, :].bitcast(BF16).rearrange("p (h v) -> p h v", v=128); sbfb = [Buf("Sbf%d" % h) for h in range(8)]
            for h in range(8):
                S.op("gpsimd", lambda e, h=h: e.memset(Sst[h][:], 0.0), writes=[sstb[h]])
            S.op("gpsimd", lambda e: e.memset(stat3[:, :], 0.0), writes=sbfb)
            sm = sb("sm", [128, 64]); smr = [0]
            ba_t = sb("ba_t", [128, 16]); ba_b = Buf("ba_t")
            GLs_t = hist_s[:, 0:4, :].rearrange("p c k -> p (c k)")
            PREh = chs[:].rearrange("p c s j -> p (c s j)")[:, 0:1152].rearrange("p (c k) -> p c k", k=48)

            smr2 = [0, 0, 0]

            def SM(w=8, par_=None):
                if par_ is None:
                    i = smr[0] % 8
                    smr[0] += 1
                    return sm[:, i * 8:i * 8 + w], smb[i]
                i = par_ * 2 + smr2[par_] % 2
                smr2[par_] += 1
                return sm[:, i * 8:i * 8 + w], smb[i]
            smb = [Buf("sm%d" % i) for i in range(8)]
            sm2 = sb("sm2", [128, 8, 24]); sm2b = Buf("sm2")
            presb = [Buf("pres%d" % c) for c in range(24)]
            for q in range(6):
                sct, sctb = xt_ring.next()
                S.dma("sync", sct[0:48, 0:512], scb[:, q * 512:(q + 1) * 512], writes=[sctb])
                pp, ppb = next_ps(4, 6)
                for c in range(4):
                    S.op("tensor", lambda e, c=c, pp=pp, sct=sct: e.transpose(pp[:, c * 48:(c + 1) * 48], sct[0:48, c * 128:(c + 1) * 128], ident[0:48, 0:48]),
                         reads=[sctb, ident_b], writes=[ppb])
                S.op("vector", lambda e, q=q, pp=pp: e.tensor_copy(out=PREh[:, q * 4:(q + 1) * 4, :], in_=pp[:, 0:192].rearrange("p (c k) -> p c k", k=48)),
                     reads=[ppb], writes=presb[q * 4:(q + 1) * 4])
            SINb = xt_ring.items[0][0][:, :].bitcast(BF16).rearrange("p (s v) -> p s v", v=128)
            sinbb = xt_ring.items[0][1]
            xt_ring.items.pop(0)
            ctx = dict(l=l, base=base, T=T, TB=TB, TI=TI, NPAR=NPAR, bufs_f1=bufs_f[1], bufs_x0=(bufs_b[2] + [b_ for (_t, b_) in pre_extra]), masks=masks, mb=mb, ones_f=ones_f, ones_fb=ones_fb, rowmask=rowmask, ident_bf=ident_bf,
                       prm=prm, prmb=prmb, HIST=HIST, histb=histb, PRE=PRE, ipbank=ipbank, miscbank=miscbank, chainbank=chainbank, Sst=Sst, sstb=sstb, Sbf=Sbf, sbfb=sbfb, SIN=SIN, sinb=sinb, SINb=SINb, sinbb=sinbb, WM=WM, ba_t=ba_t, ba_b=ba_b, GLs_t=GLs_t, PREh=PREh,
                       SM=SM, sm=sm, smb=smb, sm2=sm2, sm2b=sm2b, presb=presb)
            for ti in range(17):
                l1_tile(ti, ctx)
                ck(100 + ti)
            for q in range(6):
                pp, ppb = next_ps(4, 6)
                for c in range(4):
                    S.op("tensor", lambda e, c=c, q=q, pp=pp: e.transpose(pp[0:3, c * 128:(c + 1) * 128], HIST[:, q * 4 + c, :], ident[:]),
                         reads=[histb[q * 4 + c], ident_b], writes=[ppb])
                S.op("vector", lambda e, q=q, pp=pp: e.tensor_copy(out=small_o[0:3, (q % 2) * 512:(q % 2 + 1) * 512], in_=pp[0:3, :]), reads=[ppb], writes=[small_o_b])
                if q % 2 == 1:
                    S.dma("sync", cbp[:, (q // 2) * 1024:(q // 2 + 1) * 1024], small_o[0:3, :], reads=[small_o_b], is_output=True)
            for h in range(8):
                S.dma("sync", ssp[h], Sst[h][:], reads=[sstb[h]], is_output=True)
            for q in range(6):
                pp, ppb = next_ps(4, 6)
                for c in range(4):
                    S.op("tensor", lambda e, c=c, q=q, pp=pp: e.transpose(pp[0:48, c * 128:(c + 1) * 128], PREh[:, q * 4 + c, :], ident[:]),
                         reads=[presb[q * 4 + c], ident_b], writes=[ppb])
                S.op("vector", lambda e, q=q, pp=pp: e.tensor_copy(out=small_o[0:48, (q % 2) * 512:(q % 2 + 1) * 512], in_=pp[0:48, :]), reads=[ppb], writes=[small_o_b])
                if q % 2 == 1:
                    S.dma("sync", cbs[:, (q // 2) * 1024:(q // 2 + 1) * 1024], small_o[0:48, :], reads=[small_o_b], is_output=True)

        def l1_tile(ti, X):
            l = 1
            gi = X["base"] + ti
            kind, t0, n, col0 = groups[gi]
            T, TB, TI, masks, mb = X["T"], X["TB"], X["TI"], X["masks"], X["mb"]
            prm, prmb, SM = X["prm"], X["prmb"], X["SM"]
            MUI, MLS, MUS = masks[("MUI", kind)], masks[("MLS", kind)], masks[("MUS", kind)]
            ones_f, ones_fb = X["ones_f"], X["ones_fb"]
            make_u(l, gi)
            pb_, pbb = next_ps(0, 4)
            inproj(32, 128, pb_, pbb, mw=16)
            bafm, bafmb = T()
            S.op("vector", lambda e: e.tensor_copy(out=bafm[0:16, :], in_=pb_[0:16, 0:128]), reads=[pbb], writes=[bafmb])
            pq, pqb = next_ps(6, 8)
            S.op("tensor", lambda e: e.transpose(pq[:, 0:16], bafm[0:16, :], ident[0:16, 0:16]), reads=[bafmb, ident_b], writes=[pqb])
            sm2, sm2b = X["sm2"], X["sm2b"]
            ba, bab = X["ba_t"], X["ba_b"]
            S.op("vector", lambda e: e.tensor_copy(out=ba[:], in_=pq[:, 0:16]), reads=[pqb], writes=[bab])
            S.op("scalar", lambda e: e.activation(out=sm2[:, :, 0], in_=ba[:, 0:8], func=AF.Exp, scale=-1.0), reads=[bab], writes=[sm2b])
            S.op("vector", lambda e: e.tensor_scalar_add(out=sm2[:, :, 0], in0=sm2[:, :, 0], scalar1=1.0), reads=[sm2b], writes=[sm2b])
            S.op("scalar", lambda e: e.activation(out=sm2[:, :, 0], in_=sm2[:, :, 0], func=AF.Ln), reads=[sm2b], writes=[sm2b])
            S.op("scalar", lambda e: e.activation(out=sm2[:, :, 1], in_=sm2[:, :, 0], func=AF.Exp, scale=-0.5), reads=[sm2b], writes=[sm2b])
            S.op("scalar", lambda e: e.activation(out=sm2[:, :, 0], in_=sm2[:, :, 0], func=AF.Exp, scale=-1.0), reads=[sm2b], writes=[sm2b])
            S.op("vector", lambda e: e.tensor_tensor(out=sm2[:, :, 6], in0=ba[:, 8:16], in1=prm[:, 0:8], op=ALU.add), reads=[bab, prmb, sm2b], writes=[sm2b])
            S.op("vector", lambda e: e.tensor_scalar_min(out=sm2[:, :, 7], in0=sm2[:, :, 6], scalar1=0.0), reads=[sm2b], writes=[sm2b])
            S.op("vector", lambda e: e.tensor_scalar_max(out=ba[:, 0:8], in0=sm2[:, :, 6], scalar1=0.0), reads=[sm2b, bab], writes=[bab])
            S.op("vector", lambda e: e.tensor_tensor(out=sm2[:, :, 7], in0=ba[:, 0:8], in1=sm2[:, :, 7], op=ALU.subtract), reads=[sm2b, bab], writes=[sm2b])
            S.op("scalar", lambda e: e.activation(out=sm2[:, :, 7], in_=sm2[:, :, 7], func=AF.Exp, scale=-1.0), reads=[sm2b], writes=[sm2b])
            S.op("vector", lambda e: e.tensor_scalar_add(out=sm2[:, :, 7], in0=sm2[:, :, 7], scalar1=1.0), reads=[sm2b], writes=[sm2b])
            S.op("scalar", lambda e: e.activation(out=sm2[:, :, 7], in_=sm2[:, :, 7], func=AF.Ln), reads=[sm2b], writes=[sm2b])
            S.op("vector", lambda e: e.tensor_tensor(out=sm2[:, :, 6], in0=ba[:, 0:8], in1=sm2[:, :, 7], op=ALU.add), reads=[sm2b, bab], writes=[sm2b])
            S.op("vector", lambda e: e.tensor_tensor(out=sm2[:, :, 2], in0=sm2[:, :, 6], in1=prm[:, 16:24], op=ALU.mult), reads=[sm2b, prmb], writes=[sm2b])
            gq, gqb = X["sm"][:, (6 + ti % 2) * 8:(6 + ti % 2) * 8 + 8], X["smb"][6 + ti % 2]
            S.op("vector", lambda e: e.tensor_copy(out=gq, in_=sm2[:, :, 2]), reads=[sm2b], writes=[gqb])
            pg, pgb = next_ps(6, 8)
            S.op("tensor", lambda e: e.matmul(pg[:, 0:8], lhsT=MUI[:], rhs=gq, start=True, stop=True), reads=[mb, gqb], writes=[pgb])
            S.op("tensor", lambda e: e.matmul(pg[:, 8:16], lhsT=MLS[:], rhs=gq, start=True, stop=True), reads=[mb, gqb], writes=[pgb])
            S.op("vector", lambda e: e.tensor_copy(out=sm2[:, :, 3], in_=pg[:, 0:8]), reads=[pgb, sm2b], writes=[sm2b])
            S.op("scalar", lambda e: e.activation(out=sm2[:, :, 4], in_=pg[:, 0:8], func=AF.Exp), reads=[pgb, sm2b], writes=[sm2b])
            S.op("scalar", lambda e: e.activation(out=sm2[:, :, 5], in_=pg[:, 8:16], func=AF.Exp), reads=[pgb, sm2b], writes=[sm2b])
            if kind == "s":
                gr, grb = T()
                S.op("vector", lambda e: e.tensor_tensor(out=gr[:].rearrange("p (h s) -> p h s", s=16), in0=gq.unsqueeze(2).to_broadcast([128, 8, 16]),
                                                         in1=X["rowmask"][:].unsqueeze(1).to_broadcast([128, 8, 16]), op=ALU.mult), reads=[gqb, mb], writes=[grb])
                pgl, pglb = next_ps(6, 8)
                S.op("tensor", lambda e: e.matmul(pgl[:, 0:128], lhsT=ones_f[:], rhs=gr[:], start=True, stop=True), reads=[ones_fb, grb], writes=[pglb])
                GLs, GLsb = X["GLs_t"], Buf("GLs")
                S.op("scalar", lambda e: e.activation(out=GLs, in_=pgl[:, 0:128], func=AF.Exp), reads=[pglb], writes=[GLsb])
            Yd = dict(GLs=(GLs, GLsb) if kind == "s" else None)
            npar = X["NPAR"] if kind == "p" else 1
            if npar == 1:
                for h_ in range(8):
                    for _ in l1_head(ti, h_, X, gi, Yd, 0):
                        pass
            else:
                for h0 in range(0, 8, npar):
                    recs = []
                    for p_ in range(min(npar, 8 - h0)):
                        S.rec = []
                        for _ in l1_head(ti, h0 + p_, X, gi, Yd, p_):
                            pass
                        recs.append(S.rec)
                        S.rec = None
                    S.replay(recs)
            outproj_ln(l, gi, True)

        def l1_head(ti, h, X, gi, Y, par):
            l = 1
            kind, t0, n, col0 = groups[gi]
            masks, mb = X["masks"], X["mb"]
            T = lambda: X["T"](par)
            TB = lambda: X["TB"](par)
            TI = lambda: X["TI"](par)
            misc = lambda: X["miscbank"](par)
            prm, prmb, SM, sm2, sm2b = X["prm"], X["prmb"], X["SM"], X["sm2"], X["sm2b"]
            MUI, MLS, MUS = masks[("MUI", kind)], masks[("MLS", kind)], masks[("MUS", kind)]
            ones_f, ones_fb, ident_bf = X["ones_f"], X["ones_fb"], X["ident_bf"]
            HIST, histb, PREh, presb, WM = X["HIST"], X["histb"], X["PREh"], X["presb"], X["WM"]

            def col(k):
                return sm2[:, h, k:k + 1]
            fm = []
            st1 = []
            for j3 in range(3):
                chn = j3 * 8 + h
                pc, pcb = X["ipbank"](par)
                inproj(chn, 128, pc, pcb)
                acc, accb = T()
                pre, preb = X["PRE"](kind, par)
                if kind == "p":
                    S.op("gpsimd", lambda e, pre=pre, chn=chn: e.tensor_copy(out=pre[:, 0:3], in_=HIST[:, chn, :]), reads=[histb[chn]], writes=[preb])
                    S.op("scalar", lambda e, pre=pre, pc=pc: e.copy(out=pre[:, 3:131], in_=pc[:, 0:128]), reads=[pcb], writes=[preb])
                    S.op("gpsimd", lambda e, pre=pre, chn=chn: e.tensor_copy(out=HIST[:, chn, :], in_=pre[:, 128:131]), reads=[preb], writes=[histb[chn]])
                    taps = [pre[:, j:j + 128] for j in range(4)]
                    av = acc[:]
                else:
                    prv = pre[:, 0:176].rearrange("p (s j) -> p s j", j=11)
                    S.op("vector", lambda e, prv=prv, chn=chn: e.tensor_copy(out=prv[:, :, 0:3], in_=PREh[:, chn, :].rearrange("p (s j) -> p s j", j=3)),
                         reads=[presb[chn]], writes=[preb])
                    S.op("scalar", lambda e, pc=pc, prv=prv: e.copy(out=prv[:, :, 3:11], in_=pc[:, 0:128].rearrange("p (s t) -> p s t", t=8)),
                         reads=[pcb], writes=[preb])
                    S.op("vector", lambda e, prv=prv, chn=chn: e.tensor_copy(out=PREh[:, chn, :].rearrange("p (s j) -> p s j", j=3), in_=prv[:, :, 8:11]),
                         reads=[preb], writes=[presb[chn]])
                    taps = [prv[:, :, j:j + 8] for j in range(4)]
                    av = acc[:].rearrange("p (s t) -> p s t", t=8)
                S.op("scalar", lambda e, av=av, taps=taps, chn=chn: e.activation(out=av, in_=taps[0], func=AF.Identity, scale=wbconv(0, chn)), reads=[preb, pb_b], writes=[accb])
                st1.append((acc, accb, av, taps, chn, preb))
            pz, pzb = X["ipbank"](par)
            inproj(24 + h, 128, pz, pzb)
            sz, szb_ = T()
            S.op("scalar", lambda e: e.activation(out=sz[:], in_=pz[:, 0:128], func=AF.Silu), reads=[pzb], writes=[szb_])
            yield
            for (acc, accb, av, taps, chn, preb) in st1:
                for j in range(1, 4):
                    S.op("vector", lambda e, av=av, taps=taps, chn=chn, j=j: e.scalar_tensor_tensor(out=av, in0=taps[j], scalar=wbconv(j, chn), in1=av, op0=ALU.mult, op1=ALU.add),
                         reads=[preb, pb_b, accb], writes=[accb])
            yield
            for (acc, accb, av, taps, chn, preb) in st1:
                S.op("scalar", lambda e, acc=acc: e.activation(out=acc[:], in_=acc[:], func=AF.Silu), reads=[accb], writes=[accb])
                fm.append((acc, accb))
            (qT, qTb), (kT, kTb), (vT, vTb) = fm
            if ti == 0 and h == 0:
                dbg('qT', qT[:], qTb); dbg('kT', kT[:], kTb); dbg('vT', vT[:], vTb); dbg('sm2', sm2[:].rearrange('p h k -> p (h k)')[:, 0:128], sm2b)
            yield
            R, Rb = T()
            S.op("vector", lambda e: e.tensor_scalar_mul(out=R[:], in0=MUI[:], scalar1=col(2)), reads=[mb, sm2b], writes=[Rb])
            pG, pGb = misc()
            S.op("tensor", lambda e: e.matmul(pG[:, 0:128], lhsT=ones_f[:], rhs=R[:], start=True, stop=True), reads=[ones_fb, Rb], writes=[pGb])
            tA, tAb = T()
            S.op("vector", lambda e: e.tensor_scalar(out=tA[:], in0=pG[:, 0:128], scalar1=col(3), scalar2=0.0, op0=ALU.subtract, op1=ALU.max), reads=[pGb, sm2b], writes=[tAb])
            S.op("scalar", lambda e: e.activation(out=tA[:], in_=tA[:], func=AF.Exp, scale=-1.0), reads=[tAb], writes=[tAb])
            S.op("vector", lambda e: e.tensor_tensor(out=tA[:], in0=tA[:], in1=MLS[:], op=ALU.mult), reads=[tAb, mb], writes=[tAb])
            tB, tBb = T()
            S.op("vector", lambda e: e.tensor_scalar(out=tB[:], in0=pG[:, 0:128], scalar1=col(3), scalar2=0.0, op0=ALU.subtract, op1=ALU.min), reads=[pGb, sm2b], writes=[tBb])
            S.op("scalar", lambda e: e.activation(out=tB[:], in_=tB[:], func=AF.Exp), reads=[tBb], writes=[tBb])
            ETi, ETib = T()
            S.op("vector", lambda e: e.tensor_tensor(out=ETi[:], in0=tB[:], in1=MUI[:], op=ALU.mult), reads=[tBb, mb], writes=[ETib])
            S.op("vector", lambda e: e.tensor_tensor(out=tB[:], in0=tB[:], in1=MUS[:], op=ALU.mult), reads=[tBb, mb], writes=[tBb])
            if ti == 0 and h == 0:
                dbg('Em', tA[:], tAb); dbg('ETs', tB[:], tBb); dbg('ETi', ETi[:], ETib)
            eG, eGb = T()
            S.op("scalar", lambda e: e.activation(out=eG[:], in_=pG[:, 0:128], func=AF.Exp), reads=[pGb], writes=[eGb])
            yield
            pk, pkb = misc()
            S.op("tensor", lambda e: e.transpose(pk[:, 0:128], kT[:], ident[:]), reads=[kTb, ident_b], writes=[pkb])
            S.op("tensor", lambda e: e.transpose(pk[:, 128:256], vT[:], ident[:]), reads=[vTb, ident_b], writes=[pkb])
            sc, scb_ = SM(8, par)
            junk, junkb = T()
            S.op("scalar", lambda e: e.activation(out=junk[:], in_=pk[:, 0:128], func=AF.Square, accum_out=sc[:, 0:1]), reads=[pkb], writes=[junkb, scb_])
            S.op("vector", lambda e: e.tensor_scalar_add(out=sc[:, 0:1], in0=sc[:, 0:1], scalar1=NORM_EPS), reads=[scb_], writes=[scb_])
            S.op("scalar", lambda e: e.activation(out=sc[:, 0:1], in_=sc[:, 0:1], func=AF.Ln), reads=[scb_], writes=[scb_])
            S.op("scalar", lambda e: e.activation(out=sc[:, 0:1], in_=sc[:, 0:1], func=AF.Exp, scale=-0.5), reads=[scb_], writes=[scb_])
            S.op("vector", lambda e: e.tensor_tensor(out=sc[:, 1:2], in0=sc[:, 0:1], in1=col(1), op=ALU.mult), reads=[scb_, sm2b], writes=[scb_])
            kc, kcb = T()
            S.op("vector", lambda e: e.tensor_scalar_mul(out=kc[:], in0=pk[:, 0:128], scalar1=sc[:, 1:2]), reads=[pkb, scb_], writes=[kcb])
            if ti == 0 and h == 0:
                dbg('kc', kc[:], kcb); dbg('eG', eG[:], eGb)
            kcbf, kcbfb = TB()
            S.op("scalar", lambda e: e.copy(out=kcbf[:], in_=kc[:]), reads=[kcb], writes=[kcbfb])
            Rk, Rkb = TB()
            S.op("vector", lambda e: e.tensor_scalar_mul(out=Rk[:], in0=kc[:], scalar1=col(4)), reads=[kcb, sm2b], writes=[Rkb])
            ktc, ktcb = TB()
            S.op("vector", lambda e: e.tensor_scalar_mul(out=ktc[:], in0=kc[:], scalar1=col(5)), reads=[kcb, sm2b], writes=[ktcb])
            Rv, Rvb = TB()
            S.op("vector", lambda e: e.tensor_scalar_mul(out=Rv[:], in0=pk[:, 128:256], scalar1=col(1)), reads=[pkb, sm2b], writes=[Rvb])
            yield
            pt2, pt2b = misc()
            S.op("tensor", lambda e: e.transpose(pt2[:, 0:64].bitcast(BF16), kcbf[:], ident_bf[:]), reads=[kcbfb, mb], writes=[pt2b])
            kcT, kcTb = TB()
            S.op("vector", lambda e: e.tensor_copy(out=kcT[:], in_=pt2[:, 0:64].bitcast(BF16)), reads=[pt2b], writes=[kcTb])
            if ti == 0 and h == 0:
                dbg('kcT', kcT[:], kcTb); dbg('Rv', Rv[:], Rvb)
            qbf, qbfb = TB()
            S.op("scalar", lambda e: e.copy(out=qbf[:], in_=qT[:]), reads=[qTb], writes=[qbfb])
            qg, qgb = TB()
            S.op("vector", lambda e: e.tensor_tensor(out=qg[:], in0=qT[:], in1=eG[:], op=ALU.mult), reads=[qTb, eGb], writes=[qgb])
            qsq, qsqb = TB()
            S.op("scalar", lambda e: e.activation(out=qsq[:], in_=qT[:], func=AF.Square), reads=[qTb], writes=[qsqb])
            pqq, pqqb = misc()
            S.op("tensor", lambda e: e.matmul(pqq[:, 0:1], lhsT=qsq[:], rhs=ones_bf[:, 0:1], start=True, stop=True), reads=[qsqb, ones_b], writes=[pqqb])
            S.op("vector", lambda e: e.tensor_scalar_add(out=sc[:, 2:3], in0=pqq[:, 0:1], scalar1=NORM_EPS), reads=[pqqb, scb_], writes=[scb_])
            S.op("scalar", lambda e: e.activation(out=sc[:, 2:3], in_=sc[:, 2:3], func=AF.Ln), reads=[scb_], writes=[scb_])
            S.op("scalar", lambda e: e.activation(out=sc[:, 2:3], in_=sc[:, 2:3], func=AF.Exp, scale=-0.5), reads=[scb_], writes=[scb_])
            S.op("vector", lambda e: e.tensor_scalar_mul(out=sc[:, 3:4], in0=sc[:, 2:3], scalar1=128.0 ** -0.5), reads=[scb_], writes=[scb_])
            yield
            pkk, pkkb = misc()
            S.op("tensor", lambda e: e.matmul(pkk[:, 0:128], lhsT=kcT[:], rhs=kcT[:], start=True, stop=True), reads=[kcTb], writes=[pkkb])
            S.op("tensor", lambda e: e.matmul(pkk[:, 128:256], lhsT=kcT[:], rhs=qbf[:], start=True, stop=True), reads=[kcTb, qbfb], writes=[pkkb])
            USE_R = os.environ.get("MK_F32R", "0") == "1"
            rr = (lambda ap: ap.bitcast(F32R)) if USE_R else (lambda ap: ap)
            Xc, Xcb = TI()
            S.op("vector", lambda e, Xc=Xc: e.tensor_tensor(out=rr(Xc[:]), in0=pkk[:, 0:128], in1=tA[:], op=ALU.mult), reads=[pkkb, tAb], writes=[Xcb])
            Yc, Ycb = TI()
            S.op("vector", lambda e, Yc=Yc: e.tensor_tensor(out=rr(Yc[:]), in0=pkk[:, 0:128], in1=tB[:], op=ALU.mult), reads=[pkkb, tBb], writes=[Ycb])
            attnT, attnTb = TB()
            S.op("vector", lambda e: e.tensor_tensor(out=attnT[:], in0=pkk[:, 128:256], in1=ETi[:], op=ALU.mult), reads=[pkkb, ETib], writes=[attnTb])
            U, Ub = TI()
            S.op("vector", lambda e, U=U, Yc=Yc: e.tensor_tensor(out=rr(U[:]), in0=ident[:], in1=Yc[:], op=ALU.subtract), reads=[ident_b, mb, Ycb], writes=[Ub])
            yield
            nlev = 6 if kind == "p" else 2
            for j in range(1, nlev + 1):
                pl, plb = misc()
                S.op("tensor", lambda e, pl=pl, Xc=Xc, Yc=Yc: e.matmul(pl[:, 0:128], lhsT=rr(Yc[:]), rhs=rr(Xc[:]), start=True, stop=True), reads=[Xcb, Ycb], writes=[plb])
                if j < nlev:
                    S.op("tensor", lambda e, pl=pl, Xc=Xc, Yc=Yc: e.matmul(pl[:, 128:256], lhsT=rr(Xc[:]), rhs=rr(Yc[:]), start=True, stop=True), reads=[Xcb, Ycb], writes=[plb])
                Xn, Xnb = TI()
                S.op("scalar", lambda e, pl=pl, Xn=Xn: e.copy(out=rr(Xn[:]), in_=pl[:, 0:128]), reads=[plb], writes=[Xnb])
                if j < nlev:
                    Yn, Ynb = TI()
                    S.op("scalar", lambda e, pl=pl, Yn=Yn: e.copy(out=rr(Yn[:]), in_=pl[:, 128:256]), reads=[plb], writes=[Ynb])
                else:
                    Yn, Ynb = Yc, Ycb
                yield
                S.op("tensor", lambda e, pl=pl, Xn=Xn, U=U: e.matmul(pl[:, 256:384], lhsT=rr(Xn[:]), rhs=rr(U[:]), start=True, stop=True), reads=[Xnb, Ub], writes=[plb])
                Un, Unb = TI()
                S.op("vector", lambda e, pl=pl, Un=Un, U=U: e.tensor_tensor(out=rr(Un[:]), in0=pl[:, 256:384], in1=U[:], op=ALU.add), reads=[plb, Ub], writes=[Unb])
                Xc, Xcb, Yc, Ycb, U, Ub = Xn, Xnb, Yn, Ynb, Un, Unb
                yield
            if INV_DT == F32:
                Ubf, Ubfb = TB()
                S.op("scalar", lambda e, U=U: e.copy(out=Ubf[:], in_=U[:]), reads=[Ub], writes=[Ubfb])
            else:
                Ubf, Ubfb = U, Ub
            pu, pub = misc()
            S.op("tensor", lambda e: e.matmul(pu[:, 0:128], lhsT=Ubf[:], rhs=Rv[:], start=True, stop=True), reads=[Ubfb, Rvb], writes=[pub])
            S.op("tensor", lambda e: e.matmul(pu[:, 128:256], lhsT=Rk[:], rhs=Ubf[:], start=True, stop=True), reads=[Ubfb, Rkb], writes=[pub])
            up, upb = T()
            S.op("scalar", lambda e: e.copy(out=up[:], in_=pu[:, 0:128]), reads=[pub], writes=[upb])
            wT, wTb = TB()
            S.op("vector", lambda e: e.tensor_copy(out=wT[:], in_=pu[:, 128:256]), reads=[pub], writes=[wTb])
            yield
            if ti == 0 and h == 0:
                dbg('U', Ubf[:], Ubfb); dbg('up', up[:], upb); dbg('wT', wT[:], wTb)
            Sst, sstb, Sbf, sbfb = X["Sst"], X["sstb"], X["Sbf"], X["sbfb"]
            pc1, pc1b = X["chainbank"](par)
            pc2v, pc2b = pc1[:, 256:384], pc1b
            vn, vnb = TB()
            if kind == "p":
                S.op("tensor", lambda e: e.matmul(pc1[:, 0:128], lhsT=wT[:], rhs=Sbf[:, h, :], start=True, stop=True), reads=[wTb, sbfb[h]], writes=[pc1b])
                S.op("vector", lambda e: e.tensor_tensor(out=vn[:], in0=up[:], in1=pc1[:, 0:128], op=ALU.subtract), reads=[upb, pc1b], writes=[vnb])
                yield
                S.op("tensor", lambda e: e.matmul(pc2v, lhsT=qg[:], rhs=Sbf[:, h, :], start=True, stop=False), reads=[qgb, sbfb[h]], writes=[pc2b])
                S.op("tensor", lambda e: e.matmul(pc2v, lhsT=attnT[:], rhs=vn[:], start=False, stop=True), reads=[attnTb, vnb], writes=[pc2b])
                S.op("tensor", lambda e: e.matmul(pc1[:, 128:256], lhsT=ktc[:], rhs=vn[:], start=True, stop=True), reads=[ktcb, vnb], writes=[pc1b])
                S.op("vector", lambda e: e.tensor_copy(out=sc[:, 6:7], in_=eG[:, 127:128]), reads=[eGb, scb_], writes=[scb_])
                S.op("vector", lambda e: e.scalar_tensor_tensor(out=Sst[h][:], in0=Sst[h][:], scalar=sc[:, 6:7], in1=pc1[:, 128:256], op0=ALU.mult, op1=ALU.add),
                     reads=[sstb[h], scb_, pc1b], writes=[sstb[h]])
                S.op("scalar", lambda e: e.copy(out=Sbf[:, h, :], in_=Sst[h][:]), reads=[sstb[h]], writes=[sbfb[h]])
                if ti == 0 and h == 0:
                    dbg('vn', vn[:], vnb); dbg('S1', Sst[h][:], sstb[h])
            else:
                SIN, sinb, SINb, sinbb = X["SIN"], X["sinb"], X["SINb"], X["sinbb"]
                GLs, GLsb = Y["GLs"]
                rowmask = X["rowmask"]
                S.dma("sync", SIN[:], ssm[:, h].rearrange("s d v -> d s v"), writes=[sinb] + X["bufs_f1"])
                S.op("scalar", lambda e: e.copy(out=SINb[:], in_=SIN[:]), reads=[sinb], writes=[sinbb] + X["bufs_x0"])
                for s_ in range(16):
                    wm_, wmb_ = WM()
                    S.op("gpsimd", lambda e, wm_=wm_: e.memset(wm_[:], 0.0), writes=[wmb_])
                    S.op("vector", lambda e, wm_=wm_, s_=s_: e.tensor_copy(out=wm_[:, s_ * 8:s_ * 8 + 8], in_=wT[:, s_ * 8:s_ * 8 + 8]), reads=[wTb], writes=[wmb_])
                    S.op("tensor", lambda e, wm_=wm_, s_=s_: e.matmul(pc1[:, 0:128], lhsT=wm_[:], rhs=SINb[:, s_, :], start=(s_ == 0), stop=(s_ == 15)),
                         reads=[wmb_, sinbb], writes=[pc1b])
                S.op("vector", lambda e: e.tensor_tensor(out=vn[:], in0=up[:], in1=pc1[:, 0:128], op=ALU.subtract), reads=[upb, pc1b], writes=[vnb])
                for s_ in range(16):
                    wm_, wmb_ = WM()
                    S.op("gpsimd", lambda e, wm_=wm_: e.memset(wm_[:], 0.0), writes=[wmb_])
                    S.op("vector", lambda e, wm_=wm_, s_=s_: e.tensor_copy(out=wm_[:, s_ * 8:s_ * 8 + 8], in_=qg[:, s_ * 8:s_ * 8 + 8]), reads=[qgb], writes=[wmb_])
                    S.op("tensor", lambda e, wm_=wm_, s_=s_: e.matmul(pc2v, lhsT=wm_[:], rhs=SINb[:, s_, :], start=(s_ == 0), stop=False),
                         reads=[wmb_, sinbb], writes=[pc2b])
                S.op("tensor", lambda e: e.matmul(pc2v, lhsT=attnT[:], rhs=vn[:], start=False, stop=True), reads=[attnTb, vnb], writes=[pc2b])
                for s_ in range(16):
                    wm_, wmb_ = WM()
                    S.op("vector", lambda e, wm_=wm_, s_=s_: e.tensor_scalar_mul(out=wm_[:], in0=ktc[:], scalar1=rowmask[:, s_:s_ + 1]), reads=[ktcb, mb], writes=[wmb_])
                    pss, pssb = misc()
                    S.op("tensor", lambda e, wm_=wm_, pss=pss: e.matmul(pss[:, 0:128], lhsT=wm_[:], rhs=vn[:], start=True, stop=True), reads=[wmb_, vnb], writes=[pssb])
                    S.op("vector", lambda e, s_=s_, pss=pss: e.scalar_tensor_tensor(out=SIN[:, s_, :], in0=SIN[:, s_, :], scalar=GLs[:, h * 16 + s_:h * 16 + s_ + 1], in1=pss[:, 0:128],
                                                                                      op0=ALU.mult, op1=ALU.add), reads=[sinb, GLsb, pssb], writes=[sinb])
                S.dma("sync", sss[:, h].rearrange("s d v -> d s v"), SIN[:], reads=[sinb], is_output=True)
            yield
            S.op("scalar", lambda e: e.activation(out=junk[:], in_=pc2v, func=AF.Square, accum_out=sc[:, 4:5]), reads=[pc2b, junkb], writes=[junkb, scb_])
            S.op("vector", lambda e: e.tensor_tensor(out=sc[:, 7:8], in0=sc[:, 3:4], in1=sc[:, 3:4], op=ALU.mult), reads=[scb_], writes=[scb_])
            S.op("vector", lambda e: e.tensor_tensor(out=sc[:, 4:5], in0=sc[:, 4:5], in1=sc[:, 7:8], op=ALU.mult), reads=[scb_], writes=[scb_])
            S.op("vector", lambda e: e.tensor_scalar(out=sc[:, 4:5], in0=sc[:, 4:5], scalar1=1.0 / 128.0, scalar2=NORM_EPS, op0=ALU.mult, op1=ALU.add), reads=[scb_], writes=[scb_])
            S.op("scalar", lambda e: e.activation(out=sc[:, 4:5], in_=sc[:, 4:5], func=AF.Ln), reads=[scb_], writes=[scb_])
            S.op("scalar", lambda e: e.activation(out=sc[:, 4:5], in_=sc[:, 4:5], func=AF.Exp, scale=-0.5), reads=[scb_], writes=[scb_])
            S.op("vector", lambda e: e.tensor_tensor(out=sc[:, 5:6], in0=sc[:, 4:5], in1=sc[:, 3:4], op=ALU.mult), reads=[scb_], writes=[scb_])
            on, onb = T()
            S.op("vector", lambda e: e.scalar_tensor_tensor(out=on[:], in0=pc2v, scalar=sc[:, 5:6], in1=prm[:, 32:160], op0=ALU.mult, op1=ALU.mult),
                 reads=[pc2b, scb_, prmb], writes=[onb])
            if ti == 0 and h == 0:
                dbg('on', on[:], onb); dbg('sc', sc, scb_, ncols=8)
            po, pob = misc()
            S.op("tensor", lambda e: e.transpose(po[:, 0:128], on[:], ident[:]), reads=[onb, ident_b], writes=[pob])
            S.op("vector", lambda e: e.tensor_tensor(out=YT[:, h, 0:128], in0=po[:, 0:128], in1=sz[:], op=ALU.mult), reads=[pob, szb_], writes=[yt_b])


        try:
            layer0()
            layer0_tail()
            if depth_run >= 2:
                layer1()
        except _Stop:
            pass
        S.finish("sync")
        S.emit(block)
        build.stats = S.stats()
    return nc


_NC_CACHE = {}


def kernel(x_prompt, x_sample, state_conv_a, state_conv_b, state_ssm_b, c_prompt, c_sample,
           w_mod, b_mod, ln_g, ln_b, wa_in, wa_conv, wa_out,
           wb_in, wb_conv, wb_a_log, wb_dt_bias, wb_norm, wb_out, _depth_run=2):
    f = lambda a: np.ascontiguousarray(np.asarray(a, dtype=np.float32))
    x_prompt, x_sample = f(x_prompt), f(x_sample)
    state_conv_a, state_conv_b, state_ssm_b = f(state_conv_a), f(state_conv_b), f(state_ssm_b)
    c_prompt, c_sample = f(c_prompt), f(c_sample)
    shared = {
        "w_mod": f(w_mod), "b_mod": f(b_mod), "ln_g": f(ln_g), "ln_b": f(ln_b),
        "wa_in": f(wa_in)[0], "wa_conv": f(wa_conv)[0], "wa_out": f(wa_out)[0],
        "wb_in": f(wb_in)[0], "wb_conv": f(wb_conv)[0], "wb_a_log": f(wb_a_log), "wb_dt_bias": f(wb_dt_bias),
        "wb_norm": f(wb_norm), "wb_out": f(wb_out)[0],
    }
    in_maps = []
    for i in range(NCORES):
        sl = slice(i * 16, (i + 1) * 16)
        m = dict(shared)
        m["xp"] = x_prompt[i]
        m["xs"] = x_sample[sl].reshape(NT_S, D)
        m["cc"] = np.concatenate([c_prompt[i:i + 1], c_sample[sl]], axis=0)
        m["sca"] = state_conv_a[0, sl].reshape(32, D)
        m["scb"] = state_conv_b[0, sl].reshape(48, 3072)
        m["ssm"] = state_ssm_b[0, sl]
        in_maps.append(m)
    if _depth_run not in _NC_CACHE:
        _NC_CACHE[_depth_run] = build(_depth_run)
    nc = _NC_CACHE[_depth_run]
    res = run_bass_kernel_spmd(nc, in_maps, core_ids=list(range(NCORES)))
    R = res.results
    kernel.last_results = R
    y_prompt = np.stack([R[i]["yp"] for i in range(NCORES)], axis=0)
    y_sample = np.concatenate([R[i]["ys"].reshape(16, 8, D) for i in range(NCORES)], axis=0)
    conv_a_p = np.stack([R[i]["cap"] for i in range(NCORES)], axis=0)[None]
    conv_b_p = np.stack([R[i]["cbp"] for i in range(NCORES)], axis=0)[None]
    ssm_p = np.stack([R[i]["ssp"] for i in range(NCORES)], axis=0)[None]
    conv_a_s = np.concatenate([R[i]["cas"].reshape(16, 2, D) for i in range(NCORES)], axis=0)[None]
    conv_b_s = np.concatenate([R[i]["cbs"].reshape(16, 3, 3072) for i in range(NCORES)], axis=0)[None]
    ssm_s = np.concatenate([R[i]["sss"] for i in range(NCORES)], axis=0)[None]
    return (y_prompt, y_sample, conv_a_p, conv_b_p, ssm_p, conv_a_s, conv_b_s, ssm_s)
```

```python
import os
import numpy as np
from contextlib import ExitStack
import concourse.bass as bass
import concourse.mybir as mybir
from concourse.bass_utils import run_bass_kernel_spmd

F32 = mybir.dt.float32
F32R = mybir.dt.float32r
BF16 = mybir.dt.bfloat16
AF = mybir.ActivationFunctionType
ALU = mybir.AluOpType

NCORES = 8
D = 1024
NT_P = 2048
NT_S = 128
NTOK = NT_P + NT_S
ALPHA = 4.0 ** 0.25
LN_EPS = 1e-5
NORM_EPS = 1e-6
GDN_IN = 4112


class Buf:
    __slots__ = ("name", "w", "r", "excl")

    def __init__(self, name, excl=False):
        self.name = name
        self.w = None
        self.r = {}
        self.excl = excl


class Eng:
    def __init__(self, name):
        self.name = name
        self.ops = []
        self.count = 0
        self.known = {}
        self.hist = {}


class Sched:
    ENGS = ("tensor", "vector", "scalar", "gpsimd", "sync")

    def __init__(self, nc, stack, n_dma_sems=56, same_engine_sync=("vector", "scalar", "gpsimd")):
        self.nc = nc
        self.eng = {n: Eng(n) for n in self.ENGS}
        self.sems = {}
        for n in self.ENGS:
            self.sems[n] = stack.enter_context(nc.semaphore("s_" + n))
        self.dma_sems = []
        self.dma_pool = {"sync": [], "gpsimd": [], "scalar": []}
        for i in range(n_dma_sems):
            k = "d%d" % i
            self.sems[k] = stack.enter_context(nc.semaphore("s_" + k))
            slot = [k, 0]
            self.dma_sems.append(slot)
            self.dma_pool["gpsimd" if i < 16 else "sync"].append(slot)
        self.dma_pool["scalar"] = self.dma_pool["sync"]
        self.dma_rr = {"sync": 0, "gpsimd": 0, "scalar": 0}
        self.same = set(same_engine_sync)
        self.out_tokens = []
        self.rec = None
        self.trans = os.environ.get("MK_TRANS", "1") == "1"

    def _need(self, e, tokens):
        for k, v in sorted(tokens.items(), key=lambda kv: -kv[1]):
            if k == e.name and e.name not in self.same:
                continue
            if e.known.get(k, 0) >= v:
                continue
            e.known[k] = v
            e.ops.append(("wait", k, v))
            if self.trans and k in self.eng and k != e.name:
                h = self.eng[k].hist.get(v)
                if h:
                    for k2, v2 in h.items():
                        if e.known.get(k2, 0) < v2:
                            e.known[k2] = v2

    @staticmethod
    def _collect(reads, writes):
        tok = {}
        for b in reads:
            if b.w is not None and tok.get(b.w[0], 0) < b.w[1]:
                tok[b.w[0]] = b.w[1]
        for b in writes:
            if b.w is not None and tok.get(b.w[0], 0) < b.w[1]:
                tok[b.w[0]] = b.w[1]
            for k, v in b.r.items():
                if tok.get(k, 0) < v:
                    tok[k] = v
        return tok

    def replay(self, lists):
        self.rec = None
        idx = [0] * len(lists)
        live = True
        while live:
            live = False
            for i, lst in enumerate(lists):
                if idx[i] < len(lst):
                    it = lst[idx[i]]
                    idx[i] += 1
                    live = True
                    if it[0] == "op":
                        self.op(it[1], it[2], it[3], it[4])
                    else:
                        self.dma(it[1], it[2], it[3], it[4], it[5], it[6], **it[7])

    def op(self, eng, fn, reads=(), writes=()):
        if self.rec is not None:
            self.rec.append(("op", eng, fn, list(reads), list(writes)))
            return None
        e = self.eng[eng]
        if eng != "tensor" and any(b.excl for b in reads):
            writes = list(writes) + [b for b in reads if b.excl]
            reads = [b for b in reads if not b.excl]
        self._need(e, self._collect(reads, writes))
        e.count += 1
        e.ops.append(("ins", fn, eng, 1))
        if self.trans:
            e.hist[e.count] = dict(e.known)
        t = (eng, e.count)
        for b in reads:
            if b.r.get(eng, 0) < e.count:
                b.r[eng] = e.count
        for b in writes:
            b.w = t
            b.r = {}
        return t

    def dma(self, q, out, in_, reads=(), writes=(), is_output=False, **kw):
        if self.rec is not None:
            self.rec.append(("dma", q, out, in_, list(reads), list(writes), is_output, kw))
            return None
        e = self.eng[q]
        tok = self._collect(reads, writes)
        pool = self.dma_pool[q]
        slot = pool[self.dma_rr[q] % len(pool)]
        self.dma_rr[q] += 1
        k = slot[0]
        if slot[1] > 0:
            tok[k] = max(tok.get(k, 0), slot[1])
        self._need(e, tok)
        slot[1] += 16
        v = slot[1]
        e.ops.append(("ins", (lambda en, out=out, in_=in_, kw=kw: en.dma_start(out=out, in_=in_, **kw)), k, 16))
        t = (k, v)
        for b in reads:
            b.r[k] = v
        for b in writes:
            b.w = t
            b.r = {}
        if is_output:
            self.out_tokens.append(t)
        return t

    def finish(self, eng="sync"):
        e = self.eng[eng]
        tok = {}
        for k, v in self.out_tokens:
            tok[k] = max(tok.get(k, 0), v)
        for k, v in tok.items():
            e.ops.append(("wait", k, v))

    def emit(self, block):
        sems = self.sems

        def run(en, e):
            for o in e.ops:
                if o[0] == "wait":
                    en.wait_ge(sems[o[1]], o[2])
                else:
                    o[1](en).then_inc(sems[o[2]], o[3])

        @block.tensor
        def _(en):
            run(en, self.eng["tensor"])

        @block.vector
        def _(en):
            run(en, self.eng["vector"])

        @block.scalar
        def _(en):
            run(en, self.eng["scalar"])

        @block.gpsimd
        def _(en):
            run(en, self.eng["gpsimd"])

        @block.sync
        def _(en):
            run(en, self.eng["sync"])

    def stats(self):
        return {n: (sum(1 for o in e.ops if o[0] == "ins"), sum(1 for o in e.ops if o[0] == "wait"))
                for n, e in self.eng.items()}


class Ring:
    def __init__(self, nc, st, name, shape, dt, n):
        self.items = []
        for i in range(n):
            t = st.enter_context(nc.sbuf_tensor("%s%d" % (name, i), shape, dt))
            self.items.append((t, Buf("%s%d" % (name, i))))
        self.i = 0

    def next(self):
        it = self.items[self.i % len(self.items)]
        self.i += 1
        return it


class _Stop(Exception):
    pass


def build(depth_run=2):
    STOP = int(os.environ.get("MK_STOP", "0"))

    def ck(k):
        if STOP == k:
            raise _Stop()
    nc = bass.Bass("TRN2", target_bir_lowering=False)

    def din(name, shape):
        return nc.dram_tensor(name, shape, F32, kind="ExternalInput").ap()

    def dout(name, shape):
        return nc.dram_tensor(name, shape, F32, kind="ExternalOutput").ap()

    xp = din("xp", [NT_P, D])
    xs = din("xs", [NT_S, D])
    cc = din("cc", [17, D])
    sca = din("sca", [32, D])
    scb = din("scb", [48, 3072])
    ssm = din("ssm", [16, 8, 128, 128])
    w_mod = din("w_mod", [2, D, 3072])
    b_mod = din("b_mod", [2, 3072])
    ln_g = din("ln_g", [2, D])
    ln_b = din("ln_b", [2, D])
    wa_in = din("wa_in", [D, 4096])
    wa_conv = din("wa_conv", [3, D])
    wa_out = din("wa_out", [D, D])
    wb_in = din("wb_in", [D, GDN_IN])
    wb_conv = din("wb_conv", [4, 3072])
    wb_a_log = din("wb_a_log", [1, 8])
    wb_dt_bias = din("wb_dt_bias", [1, 8])
    wb_norm = din("wb_norm", [1, 128])
    wb_out = din("wb_out", [D, D])

    yp = dout("yp", [NT_P, D])
    ys = dout("ys", [NT_S, D])
    cap = dout("cap", [2, D])
    cbp = dout("cbp", [3, 3072])
    ssp = dout("ssp", [8, 128, 128])
    cas = dout("cas", [32, D])
    cbs = dout("cbs", [48, 3072])
    sss = dout("sss", [16, 8, 128, 128])
    DBG = int(os.environ.get("MK_DBG", "0"))
    dbg_t = dout("dbg", [24, 128, 128]) if DBG else None
    dbg_names = []
    build.dbg_names = dbg_names

    groups = [("p", g * 512, 512, g * 512) for g in range(4)] + [("s", 0, 128, NT_P)]

    with ExitStack() as st:
        def sb(name, shape, dt=F32):
            return st.enter_context(nc.sbuf_tensor(name, shape, dt))

        XR = sb("XR", [128, 8, NTOK], BF16)
        XG = sb("XG", [128, 8, 512])
        xg_b = Buf("xg")
        xr_b = [Buf("xr%d" % i) for i in range(len(groups))]
        WIN = sb("WIN", [128, 8, GDN_IN], BF16)
        win_b = [Buf("win%d" % j) for j in range(9)]
        WOUT = sb("WOUT", [128, 8, D], BF16)
        wout_b = Buf("wout")
        UT = sb("UT", [128, 8, 512], BF16)
        ut_b = Buf("ut")
        YT = sb("YT", [128, 8, 512], BF16)
        yt_b = Buf("yt")
        ident = sb("ident", [128, 128])
        ident_b = Buf("ident")
        ones_bf = sb("ones_bf", [128, 128], BF16)
        ones_b = Buf("ones")
        PA_in = sb("PA_in", [104, 128])
        PB_in = sb("PB_in", [96, 128])
        PA = sb("PA", [128, 104])
        PB = sb("PB", [128, 96])
        pa_in_b = Buf("pa_in"); pb_in_b = Buf("pb_in"); pa_b = Buf("pa"); pb_b = Buf("pb")
        csT = sb("csT", [128, 8, 17], BF16)
        cst_b = Buf("csT")
        modT = [sb("modT%d" % l, [128, 24, 17]) for l in range(2)]
        modT_b = [Buf("modT%d" % l) for l in range(2)]
        xt_ring = Ring(nc, st, "xt", [128, D], F32, 2)
        ctile, ct_b = xt_ring.next()
        w512 = Ring(nc, st, "w512", [128, 512], F32, 4)
        stat = Ring(nc, st, "stat", [128, 512], F32, 4)
        hist_a = sb("hist_a", [128, 8, 2])
        hist_a_b = Buf("hist_a")
        chs = sb("chs", [128, 8, 16, 10])
        chs_b = [Buf("chs%d" % i) for i in range(8)]
        chp = Ring(nc, st, "chp", [128, 514], F32, 2)
        hist_s = sb("hist_s", [128, 8, 32])
        hist_s_b = Buf("hist_s")
        otile = xt_ring
        small_o = sb("small_o", [48, 1024])
        small_o_b = Buf("small_o")
        mod_tm, modtm_b = XG[:].rearrange("p c t -> p (c t)"), xg_b

        ps = [st.enter_context(nc.psum_tensor("ps%d" % i, [128, 512], F32)) for i in range(8)]
        ps_b = [Buf("ps%d" % i, excl=True) for i in range(8)]
        block = st.enter_context(nc.Block())
        _same = tuple(x for x in os.environ.get("MK_SAME", "vector,scalar,gpsimd").split(",") if x)
        S = Sched(nc, st, same_engine_sync=_same)

        psrr = [0]

        def dbg(name, ap, buf, ncols=128):
            if not DBG or len(dbg_names) >= 24:
                return
            i = len(dbg_names)
            dbg_names.append(name)
            S.dma("gpsimd", dbg_t[i][:, 0:ncols], ap, reads=[buf], is_output=True)

        def next_ps(lo=0, hi=4):
            i = lo + psrr[0] % (hi - lo)
            psrr[0] += 1
            return ps[i], ps_b[i]

        S.op("gpsimd", lambda e: e.memset(ident[:], 1.0), writes=[ident_b])
        S.op("gpsimd", lambda e: e.affine_select(out=ident[:], in_=ident[:], pattern=[[-1, 128]], compare_op=ALU.is_equal,
                                                 fill=0.0, base=0, channel_multiplier=1), reads=[ident_b], writes=[ident_b])
        S.op("gpsimd", lambda e: e.memset(ones_bf[:], 1.0), writes=[ones_b])
        S.op("gpsimd", lambda e: e.memset(hist_a[:], 0.0), writes=[hist_a_b])

        def rows(dst, r0, src, nrows):
            S.dma("sync", dst[r0:r0 + nrows, :], src, writes=[pa_in_b if dst is PA_in else pb_in_b])
        for l in range(2):
            rows(PA_in, l * 8, ln_g[l].rearrange("(c p) -> c p", p=128), 8)
            rows(PA_in, 16 + l * 8, ln_b[l].rearrange("(c p) -> c p", p=128), 8)
            rows(PA_in, 56 + l * 24, b_mod[l].rearrange("(c p) -> c p", p=128), 24)
        for j in range(3):
            rows(PA_in, 32 + j * 8, wa_conv[j].rearrange("(c p) -> c p", p=128), 8)
        for j in range(4):
            rows(PB_in, j * 24, wb_conv[j].rearrange("(c p) -> c p", p=128), 24)
        S.dma("sync", ctile[0:17, :], cc, writes=[ct_b])
        pt, ptb = ps[4], ps_b[4]
        S.op("tensor", lambda e: e.transpose(pt[:, 0:104], PA_in[:, :], ident[0:104, 0:104]), reads=[pa_in_b, ident_b], writes=[ptb])
        S.op("vector", lambda e: e.tensor_copy(out=PA[:], in_=pt[:, 0:104]), reads=[ptb], writes=[pa_b])
        S.op("tensor", lambda e: e.transpose(pt[:, 128:224], PB_in[:, :], ident[0:96, 0:96]), reads=[pb_in_b, ident_b], writes=[ptb])
        S.op("vector", lambda e: e.tensor_copy(out=PB[:], in_=pt[:, 128:224]), reads=[ptb], writes=[pb_b])

        def lng(l, c):
            return PA[:, l * 8 + c:l * 8 + c + 1]

        def lnb(l, c):
            return PA[:, 16 + l * 8 + c:16 + l * 8 + c + 1]

        def waconv(j, c):
            return PA[:, 32 + j * 8 + c:32 + j * 8 + c + 1]

        def wbconv(j, c):
            return PB[:, j * 24 + c:j * 24 + c + 1]

        S.op("scalar", lambda e: e.activation(out=ctile[0:17, :], in_=ctile[0:17, :], func=AF.Silu), reads=[ct_b], writes=[ct_b])
        for c in range(8):
            S.op("tensor", lambda e, c=c: e.transpose(pt[:, 256 + c * 17:256 + (c + 1) * 17], ctile[0:17, c * 128:(c + 1) * 128], ident[0:17, 0:17]),
                 reads=[ct_b, ident_b], writes=[ptb])
        S.op("vector", lambda e: e.tensor_copy(out=csT[:], in_=pt[:, 256:256 + 136].rearrange("p (c s) -> p c s", s=17)),
             reads=[ptb], writes=[cst_b])

        wm_views = [(UT, ut_b), (YT, yt_b)]
        for l in range(depth_run):
            for j in range(6):
                wm, wmb = wm_views[j % 2]
                S.dma("gpsimd", wm[:, :, :], w_mod[l][:, j * 512:(j + 1) * 512].rearrange("(c p) n -> p c n", p=128), writes=[wmb])
                pm, pmb = next_ps(0, 4)
                for k in range(8):
                    S.op("tensor", lambda e, k=k, wm=wm, pm=pm: e.matmul(pm[0:17, :], lhsT=csT[:, k, :], rhs=wm[:, k, :], start=(k == 0), stop=(k == 7)),
                         reads=[cst_b, wmb], writes=[pmb])
                S.op("scalar", lambda e, j=j, pm=pm: e.copy(out=mod_tm[0:17, j * 512:(j + 1) * 512], in_=pm[0:17, :]), reads=[pmb], writes=[modtm_b])
            pm, pmb = ps[5], ps_b[5]
            for ch in range(24):
                S.op("tensor", lambda e, ch=ch: e.transpose(pm[:, ch * 17:(ch + 1) * 17], mod_tm[0:17, ch * 128:(ch + 1) * 128], ident[0:17, 0:17]),
                     reads=[modtm_b, ident_b], writes=[pmb])
            S.op("vector", lambda e, l=l: e.tensor_tensor(out=modT[l][:], in0=pm[:, 0:408].rearrange("p (c s) -> p c s", s=17),
                                                       in1=PA[:, 56 + l * 24:56 + l * 24 + 24].unsqueeze(2).to_broadcast([128, 24, 17]), op=ALU.add),
                 reads=[pmb, pa_b], writes=[modT_b[l]])
            S.op("vector", lambda e, l=l: e.tensor_scalar_add(out=modT[l][:, 8:16, :], in0=modT[l][:, 8:16, :], scalar1=1.0),
                 reads=[modT_b[l]], writes=[modT_b[l]])
            S.op("vector", lambda e, l=l: e.tensor_scalar_mul(out=modT[l][:, 16:24, :], in0=modT[l][:, 16:24, :], scalar1=1.0 / ALPHA),
                 reads=[modT_b[l]], writes=[modT_b[l]])

        ck(1)
        def load_win(src, ncols):
            v = src.rearrange("(c p) n -> p c n", p=128)
            nb = (ncols + 511) // 512
            for j in range(nb):
                a, b = j * 512, min(ncols, (j + 1) * 512)
                S.dma("gpsimd", WIN[:, :, a:b], v[:, :, a:b], writes=[win_b[j]])

        def load_wout(src):
            S.dma("gpsimd", WOUT[:, :, :], src.rearrange("(c p) n -> p c n", p=128), writes=[wout_b])

        def load_group(gi):
            kind, t0, n, col0 = groups[gi]
            src = xp if kind == "p" else xs
            for t in range(n // 128):
                xt, xtb = xt_ring.next()
                S.dma("sync", xt[:], src[t0 + t * 128:t0 + (t + 1) * 128, :], writes=[xtb])
                for half in range(2):
                    pp, ppb = next_ps(4, 8)
                    for q in range(4):
                        c = half * 4 + q
                        S.op("tensor", lambda e, c=c, q=q, pp=pp, xt=xt: e.transpose(pp[:, q * 128:(q + 1) * 128], xt[:, c * 128:(c + 1) * 128], ident[:]),
                             reads=[xtb, ident_b], writes=[ppb])
                    dst = XG[:, half * 4:half * 4 + 4, t * 128:(t + 1) * 128]
                    S.op("scalar" if half == 0 else "vector",
                         (lambda e, dst=dst, pp=pp: e.copy(out=dst, in_=pp[:, :].rearrange("p (c t) -> p c t", t=128))) if half == 0 else
                         (lambda e, dst=dst, pp=pp: e.tensor_copy(out=dst, in_=pp[:, :].rearrange("p (c t) -> p c t", t=128))),
                         reads=[ppb], writes=[xg_b])

        def make_u(l, gi):
            kind, t0, n, col0 = groups[gi]
            SRC, srcb, so = (XG, xg_b, 0) if l == 0 else (XR, xr_b[gi], col0)
            for c in range(8):
                if kind == "p":
                    S.op("scalar", lambda e, c=c: e.activation(out=UT[:, c, 0:n], in_=SRC[:, c, so:so + n], func=AF.Identity,
                                                               scale=modT[l][:, 8 + c, 0:1], bias=modT[l][:, c, 0:1]),
                         reads=[srcb, modT_b[l]], writes=[ut_b])
                else:
                    tmp, tmpb = w512.next()
                    S.op("vector", lambda e, c=c, tmp=tmp: e.tensor_tensor(out=tmp[:, 0:128].rearrange("p (s t) -> p s t", t=8),
                                                                           in0=SRC[:, c, so:so + 128].rearrange("p (s t) -> p s t", t=8),
                                                                           in1=modT[l][:, 8 + c, 1:17].unsqueeze(2).to_broadcast([128, 16, 8]), op=ALU.mult),
                         reads=[srcb, modT_b[l]], writes=[tmpb])
                    S.op("vector", lambda e, c=c, tmp=tmp: e.tensor_tensor(out=UT[:, c, 0:128].rearrange("p (s t) -> p s t", t=8),
                                                                           in0=tmp[:, 0:128].rearrange("p (s t) -> p s t", t=8),
                                                                           in1=modT[l][:, c, 1:17].unsqueeze(2).to_broadcast([128, 16, 8]), op=ALU.add),
                         reads=[tmpb, modT_b[l]], writes=[ut_b])

        def inproj(m, n, pm, pmb, mw=128):
            for k in range(8):
                S.op("tensor", lambda e, k=k: e.matmul(pm[0:mw, 0:n], lhsT=WIN[:, k, m * 128:m * 128 + mw], rhs=UT[:, k, 0:n], start=(k == 0), stop=(k == 7)),
                     reads=[win_b[(m * 128) // 512], ut_b], writes=[pmb])

        def outproj_ln(l, gi, last):
            kind, t0, n, col0 = groups[gi]
            eps2 = LN_EPS / (ALPHA * ALPHA)
            for m in range(8):
                pm, pmb = next_ps(0, 4)
                for k in range(8):
                    S.op("tensor", lambda e, k=k, m=m, pm=pm: e.matmul(pm[:, 0:n], lhsT=WOUT[:, k, m * 128:(m + 1) * 128], rhs=YT[:, k, 0:n], start=(k == 0), stop=(k == 7)),
                         reads=[wout_b, yt_b], writes=[pmb])
                xv = XG[:, m, 0:n]
                if l == 0:
                    res, resb = xv, xg_b
                else:
                    res, resb = XR[:, m, col0:col0 + n], xr_b[gi]
                if kind == "p":
                    S.op("vector", lambda e, m=m, pm=pm, xv=xv, res=res: e.scalar_tensor_tensor(out=xv, in0=pm[:, 0:n], scalar=modT[l][:, 16 + m, 0:1], in1=res,
                                                                                                  op0=ALU.mult, op1=ALU.add),
                         reads=[pmb, modT_b[l], resb, xg_b], writes=[xg_b])
                else:
                    tmp, tmpb = w512.next()
                    S.op("vector", lambda e, m=m, pm=pm, tmp=tmp: e.tensor_tensor(out=tmp[:, 0:128].rearrange("p (s t) -> p s t", t=8),
                                                                                   in0=pm[:, 0:128].rearrange("p (s t) -> p s t", t=8),
                                                                                   in1=modT[l][:, 16 + m, 1:17].unsqueeze(2).to_broadcast([128, 16, 8]), op=ALU.mult),
                         reads=[pmb, modT_b[l]], writes=[tmpb])
                    S.op("gpsimd", lambda e, tmp=tmp, xv=xv, res=res: e.tensor_tensor(out=xv, in0=res, in1=tmp[:, 0:128], op=ALU.add),
                         reads=[tmpb, resb, xg_b], writes=[xg_b])
            BATCH = (n == 128)
            if BATCH:
                S.op("vector", lambda e: e.tensor_copy(out=UT[:, :, 0:n], in_=XG[:, :, 0:n]), reads=[xg_b], writes=[ut_b])
                S.op("scalar", lambda e: e.activation(out=YT[:, :, 0:n], in_=XG[:, :, 0:n], func=AF.Square), reads=[xg_b], writes=[yt_b])
            for c in range(8):
                if BATCH:
                    break
                xv = XG[:, c, 0:n]
                S.op("vector", lambda e, c=c, xv=xv: e.tensor_copy(out=UT[:, c, 0:n], in_=xv), reads=[xg_b], writes=[ut_b])
                S.op("scalar", lambda e, c=c, xv=xv: e.activation(out=YT[:, c, 0:n], in_=xv, func=AF.Square), reads=[xg_b], writes=[yt_b])
            p1, p1b = next_ps(0, 4)
            p2, p2b = next_ps(0, 4)
            for c in range(8):
                S.op("tensor", lambda e, c=c, p1=p1: e.matmul(p1[:, 0:n], lhsT=ones_bf[:], rhs=UT[:, c, 0:n], start=(c == 0), stop=(c == 7)),
                     reads=[ones_b, ut_b], writes=[p1b])
            for c in range(8):
                S.op("tensor", lambda e, c=c, p2=p2: e.matmul(p2[:, 0:n], lhsT=ones_bf[:], rhs=YT[:, c, 0:n], start=(c == 0), stop=(c == 7)),
                     reads=[ones_b, yt_b], writes=[p2b])
            mean, meanb = stat.next()
            msq, msqb = stat.next()
            rstd, rstdb = stat.next()
            S.op("scalar", lambda e: e.mul(out=mean[:, 0:n], in_=p1[:, 0:n], mul=1.0 / D), reads=[p1b], writes=[meanb])
            S.op("vector", lambda e: e.tensor_tensor(out=msq[:, 0:n], in0=mean[:, 0:n], in1=mean[:, 0:n], op=ALU.mult), reads=[meanb], writes=[msqb])
            S.op("vector", lambda e: e.scalar_tensor_tensor(out=rstd[:, 0:n], in0=p2[:, 0:n], scalar=1.0 / D, in1=msq[:, 0:n], op0=ALU.mult, op1=ALU.subtract),
                 reads=[p2b, msqb], writes=[rstdb])
            S.op("vector", lambda e: e.tensor_scalar(out=rstd[:, 0:n], in0=rstd[:, 0:n], scalar1=0.0, scalar2=eps2, op0=ALU.max, op1=ALU.add),
                 reads=[rstdb], writes=[rstdb])
            S.op("scalar", lambda e: e.activation(out=rstd[:, 0:n], in_=rstd[:, 0:n], func=AF.Ln), reads=[rstdb], writes=[rstdb])
            S.op("scalar", lambda e: e.activation(out=rstd[:, 0:n], in_=rstd[:, 0:n], func=AF.Exp, scale=-0.5), reads=[rstdb], writes=[rstdb])
            if BATCH:
                x3 = XG[:, :, 0:n]
                S.op("vector", lambda e: e.tensor_tensor(out=x3, in0=x3, in1=mean[:, 0:n].unsqueeze(1).to_broadcast([128, 8, n]), op=ALU.subtract),
                     reads=[xg_b, meanb], writes=[xg_b])
                S.op("vector", lambda e: e.tensor_tensor(out=x3, in0=x3, in1=rstd[:, 0:n].unsqueeze(1).to_broadcast([128, 8, n]), op=ALU.mult),
                     reads=[xg_b, rstdb], writes=[xg_b])
            for c in range(8):
                xv = XG[:, c, 0:n]
                eng = "vector" if c % 2 == 0 else "gpsimd"
                if not BATCH:
                    S.op(eng, lambda e, xv=xv: e.tensor_tensor(out=xv, in0=xv, in1=mean[:, 0:n], op=ALU.subtract), reads=[xg_b, meanb], writes=[xg_b])
                    S.op(eng, lambda e, xv=xv: e.tensor_tensor(out=xv, in0=xv, in1=rstd[:, 0:n], op=ALU.mult), reads=[xg_b, rstdb], writes=[xg_b])
                if last:
                    S.op("scalar", lambda e, xv=xv, c=c: e.activation(out=xv, in_=xv, func=AF.Identity, scale=lng(l, c), bias=lnb(l, c)),
                         reads=[xg_b, pa_b], writes=[xg_b])
                else:
                    S.op("scalar", lambda e, xv=xv, c=c: e.activation(out=XR[:, c, col0:col0 + n], in_=xv, func=AF.Identity, scale=lng(l, c), bias=lnb(l, c)),
                         reads=[xg_b, pa_b], writes=[xr_b[gi]])
            if last:
                store_group(gi)

        def store_group(gi):
            kind, t0, n, col0 = groups[gi]
            dst = yp if kind == "p" else ys
            for t in range(n // 128):
                ot, otb = otile.next()
                for half in range(2):
                    pp, ppb = next_ps(4, 8)
                    for q in range(4):
                        c = half * 4 + q
                        S.op("tensor", lambda e, c=c, q=q, pp=pp, t=t: e.transpose(pp[:, q * 128:(q + 1) * 128], XG[:, c, t * 128:(t + 1) * 128], ident[:]),
                             reads=[xg_b, ident_b], writes=[ppb])
                    if half == 0:
                        S.op("scalar", lambda e, pp=pp, ot=ot: e.copy(out=ot[:, 0:512], in_=pp[:, :]), reads=[ppb], writes=[otb])
                    else:
                        S.op("vector", lambda e, pp=pp, ot=ot: e.tensor_copy(out=ot[:, 512:1024], in_=pp[:, :]), reads=[ppb], writes=[otb])
                S.dma("sync", dst[t0 + t * 128:t0 + (t + 1) * 128, :], ot[:], reads=[otb], is_output=True)

        def layer0():
            l = 0
            load_win(wa_in, 4096)
            load_wout(wa_out)
            sct, sctb = xt_ring.next()
            S.dma("sync", sct[0:32, :], sca, writes=[sctb])
            pp, ppb = next_ps(4, 8)
            for c in range(8):
                S.op("tensor", lambda e, c=c: e.transpose(pp[:, c * 32:(c + 1) * 32], sct[0:32, c * 128:(c + 1) * 128], ident[0:32, 0:32]),
                     reads=[sctb, ident_b], writes=[ppb])
            S.op("vector", lambda e: e.tensor_copy(out=chs[:, :, :, 0:2], in_=pp[:, 0:256].rearrange("p (c s j) -> p c s j", s=16, j=2)),
                 reads=[ppb], writes=chs_b)
            for gi in range(len(groups)):
                l0_group(gi)

        def l0_group(gi):
            if True:
                l = 0
                kind, t0, n, col0 = groups[gi]
                load_group(gi)
                ck(2)
                if gi == 4:
                    ck(30)
                make_u(l, gi)
                ck(3)
                if gi == 4:
                    ck(31)
                for i in range(8):
                    pc, pcb = next_ps(0, 4)
                    inproj(8 + i, n, pc, pcb)
                    csb, csbb = w512.next()
                    S.op("scalar", lambda e, pc=pc, csb=csb: e.copy(out=csb[:, 0:n], in_=pc[:, 0:n]), reads=[pcb], writes=[csbb])
                    ph, phb = next_ps(0, 4)
                    inproj(16 + i, n, ph, phb)
                    ysb, ysbb = w512.next()
                    if kind == "p":
                        ch, chb = chp.next()
                        S.op("gpsimd", lambda e, ch=ch, i=i: e.tensor_copy(out=ch[:, 0:2], in_=hist_a[:, i, :]), reads=[hist_a_b], writes=[chb])
                        S.op("vector", lambda e, ch=ch, csb=csb, ph=ph: e.tensor_tensor(out=ch[:, 2:2 + n], in0=csb[:, 0:n], in1=ph[:, 0:n], op=ALU.mult),
                             reads=[csbb, phb], writes=[chb])
                        S.op("gpsimd", lambda e, ch=ch, i=i: e.tensor_copy(out=hist_a[:, i, :], in_=ch[:, n:n + 2]), reads=[chb], writes=[hist_a_b])
                        taps = [ch[:, j:j + n] for j in range(3)]
                        yv = ysb[:, 0:n]
                    else:
                        chb = chs_b[i]
                        S.op("vector", lambda e, i=i, csb=csb, ph=ph: e.tensor_tensor(out=chs[:, i, :, 2:10], in0=csb[:, 0:128].rearrange("p (s t) -> p s t", t=8),
                                                                                       in1=ph[:, 0:128].rearrange("p (s t) -> p s t", t=8), op=ALU.mult),
                             reads=[csbb, phb], writes=[chb])
                        taps = [chs[:, i, :, j:j + 8] for j in range(3)]
                        yv = ysb[:, 0:128].rearrange("p (s t) -> p s t", t=8)
                    S.op("scalar", lambda e, yv=yv, taps=taps, i=i: e.activation(out=yv, in_=taps[0], func=AF.Identity, scale=waconv(0, i)),
                         reads=[chb, pa_b], writes=[ysbb])
                    S.op("vector", lambda e, yv=yv, taps=taps, i=i: e.scalar_tensor_tensor(out=yv, in0=taps[1], scalar=waconv(1, i), in1=yv, op0=ALU.mult, op1=ALU.add),
                         reads=[chb, pa_b, ysbb], writes=[ysbb])
                    S.op("vector", lambda e, yv=yv, taps=taps, i=i: e.scalar_tensor_tensor(out=yv, in0=taps[2], scalar=waconv(2, i), in1=yv, op0=ALU.mult, op1=ALU.add),
                         reads=[chb, pa_b, ysbb], writes=[ysbb])
                    pz, pzb = next_ps(0, 4)
                    inproj(24 + i, n, pz, pzb)
                    szb, szbb = w512.next()
                    S.op("scalar", lambda e, pz=pz, szb=szb: e.activation(out=szb[:, 0:n], in_=pz[:, 0:n], func=AF.Silu), reads=[pzb], writes=[szbb])
                    pb_, pbb = next_ps(0, 4)
                    inproj(i, n, pb_, pbb)
                    S.op("vector", lambda e, pb_=pb_, szb=szb: e.tensor_tensor(out=szb[:, 0:n], in0=pb_[:, 0:n], in1=szb[:, 0:n], op=ALU.mult),
                         reads=[pbb, szbb], writes=[szbb])
                    S.op("vector", lambda e, i=i, ysb=ysb, szb=szb: e.tensor_tensor(out=YT[:, i, 0:n], in0=ysb[:, 0:n], in1=szb[:, 0:n], op=ALU.mult),
                         reads=[ysbb, szbb], writes=[yt_b])
                ck(4)
                if gi == 4:
                    ck(32)
                outproj_ln(l, gi, depth_run == 1)
                ck(5)
                ck(11 + gi) if gi >= 1 else None
        def layer0_tail():
            ck(20)
            pp, ppb = next_ps(4, 8)
            for c in range(4):
                S.op("tensor", lambda e, c=c: e.transpose(pp[0:2, c * 128:(c + 1) * 128], hist_a[:, c, :], ident[:]),
                     reads=[hist_a_b, ident_b], writes=[ppb])
            S.op("vector", lambda e: e.tensor_copy(out=hist_s[:].rearrange("p c (s j) -> p c s j", j=2), in_=chs[:, :, :, 8:10]),
                 reads=chs_b, writes=[hist_s_b])
            pp2, pp2b = next_ps(4, 8)
            for c in range(4):
                S.op("tensor", lambda e, c=c: e.transpose(pp2[0:32, c * 128:(c + 1) * 128], hist_s[:, c, :], ident[:]),
                     reads=[hist_s_b, ident_b], writes=[pp2b])
            pp3, pp3b = next_ps(4, 8)
            for c in range(4, 8):
                S.op("tensor", lambda e, c=c: e.transpose(pp3[0:32, (c - 4) * 128:(c - 3) * 128], hist_s[:, c, :], ident[:]),
                     reads=[hist_s_b, ident_b], writes=[pp3b])
            S.op("vector", lambda e: e.tensor_copy(out=small_o[0:32, 0:512], in_=pp2[0:32, :]), reads=[pp2b], writes=[small_o_b])
            S.op("vector", lambda e: e.tensor_copy(out=small_o[0:32, 512:1024], in_=pp3[0:32, :]), reads=[pp3b], writes=[small_o_b])
            S.dma("sync", cas, small_o[0:32, 0:1024], reads=[small_o_b], is_output=True)
            S.op("vector", lambda e: e.tensor_copy(out=small_o[0:2, 0:512], in_=pp[0:2, :]), reads=[ppb], writes=[small_o_b])
            S.op("tensor", lambda e: e.transpose(pp[0:2, 0:128], hist_a[:, 4, :], ident[:]), reads=[hist_a_b, ident_b], writes=[ppb])
            S.op("tensor", lambda e: e.transpose(pp[0:2, 128:256], hist_a[:, 5, :], ident[:]), reads=[hist_a_b, ident_b], writes=[ppb])
            S.op("tensor", lambda e: e.transpose(pp[0:2, 256:384], hist_a[:, 6, :], ident[:]), reads=[hist_a_b, ident_b], writes=[ppb])
            S.op("tensor", lambda e: e.transpose(pp[0:2, 384:512], hist_a[:, 7, :], ident[:]), reads=[hist_a_b, ident_b], writes=[ppb])
            S.op("vector", lambda e: e.tensor_copy(out=small_o[0:2, 512:1024], in_=pp[0:2, :]), reads=[ppb], writes=[small_o_b])
            S.dma("sync", cap, small_o[0:2, 0:1024], reads=[small_o_b], is_output=True)

        INV_DT = BF16 if os.environ.get("MK_INV", "f32") == "bf16" else F32

        def barrier():
            toks = {}
            for n_, e_ in S.eng.items():
                if e_.count:
                    toks[n_] = e_.count
            for k_, v_ in S.dma_sems:
                if v_:
                    toks[k_] = v_
            for n_, e_ in S.eng.items():
                S._need(e_, dict(toks))

        def layer1():
            l = 1
            barrier()
            base = len(groups)
            for ti in range(16):
                groups.append(("p", ti * 128, 128, ti * 128))
                xr_b.append(xr_b[ti // 4])
            groups.append(("s", 0, 128, NT_P))
            xr_b.append(xr_b[4])
            load_win(wb_in, GDN_IN)
            load_wout(wb_out)
            f32_tmps = []
            for (t_, _b) in w512.items:
                for q in range(4):
                    f32_tmps.append(t_[:, q * 128:(q + 1) * 128])
            for c in range(8):
                for q in range(1, 4):
                    f32_tmps.append(XG[:, c, q * 128:(q + 1) * 128])
            bf_tmps = []
            for c in range(8):
                for q in range(1, 4):
                    bf_tmps.append(UT[:, c, q * 128:(q + 1) * 128])
                    bf_tmps.append(YT[:, c, q * 128:(q + 1) * 128])
            for (t_, _b) in stat.items[:3]:
                for q in range(1, 4):
                    f32_tmps.append(t_[:, q * 128:(q + 1) * 128])
            res_f = [f32_tmps.pop() for _ in range(15)]
            SIN = sb("SIN", [128, 16, 128]); sinb = Buf("SIN")
            NPAR = int(os.environ.get("MK_NPAR", "3"))
            pools_f = [[f32_tmps.pop() for _ in range(14)], [SIN[:, s_, :] for s_ in range(14)]]
            INVT = sb("INVT", [128, 12, 128])
            pools_i = [[INVT[:, p_ * 6 + i, :] for i in range(6)] for p_ in range(2)]
            wm_b = [bf_tmps.pop() for _ in range(4)]
            pools_b = [[bf_tmps.pop() for _ in range(18)], [bf_tmps.pop() for _ in range(18)]]
            if NPAR >= 3:
                pools_f.append([f32_tmps.pop() for _ in range(14)])
                pools_i.append([f32_tmps.pop() for _ in range(6)])
                xt0f = xt_ring.items[0][0]
                xt0 = xt0f[:, 0:576].bitcast(BF16)
                pools_b.append([xt0[:, i * 128:(i + 1) * 128] for i in range(9)] + [bf_tmps.pop() for _ in range(8)])
                pre_extra_t = [xt0f[:, 576 + i * 131:576 + (i + 1) * 131] for i in range(3)]
            cnt = {"w": 0}
            for p_ in range(3):
                cnt["f%d" % p_] = 0; cnt["b%d" % p_] = 0; cnt["i%d" % p_] = 0
            bufs_f = [[Buf("tf%d_%d" % (p_, i)) for i in range(14)] for p_ in range(3)]
            bufs_i = [[Buf("ti%d_%d" % (p_, i)) for i in range(6)] for p_ in range(3)]
            bufs_b = [[Buf("tb%d_%d" % (p_, i)) for i in range(18)] for p_ in range(3)]
            wm_bufs = [Buf("wm%d" % i) for i in range(4)]
            assert INV_DT == F32

            def T(par=0):
                k_ = "f%d" % par
                i = cnt[k_] % 14
                cnt[k_] += 1
                return pools_f[par][i], bufs_f[par][i]

            def TB(par=0):
                k_ = "b%d" % par
                i = cnt[k_] % len(pools_b[par])
                cnt[k_] += 1
                return pools_b[par][i], bufs_b[par][i]

            def TI(par=0):
                k_ = "i%d" % par
                i = cnt[k_] % 6
                cnt[k_] += 1
                return pools_i[par][i], bufs_i[par][i]

            def WM():
                i = cnt["w"] % 4
                cnt["w"] += 1
                return wm_b[i], wm_bufs[i]
            ones_f = res_f[0]; ones_fb = Buf("ones_f")
            S.op("gpsimd", lambda e: e.memset(ones_f[:], 1.0), writes=[ones_fb])
            masks = {}
            mb = Buf("masks")
            for nm, pat, base_, cm in (("MUI", [[1, 128]], 0, -1), ("MLS", [[-1, 128]], -1, 1), ("MUS", [[1, 128]], -1, -1)):
                for kd in ("p", "s"):
                    masks[(nm, kd)] = res_f[1 + len(masks)]
                S.op("gpsimd", lambda e, nm=nm, pat=pat, base_=base_, cm=cm: e.affine_select(out=masks[(nm, "p")][:], in_=ones_f[:], pattern=pat, compare_op=ALU.is_ge,
                                                                                           fill=0.0, base=base_, channel_multiplier=cm), reads=[ones_fb], writes=[mb])
            SS = sb("SS", [128, 128])
            S.op("gpsimd", lambda e: e.affine_select(out=SS[:].rearrange("p (s t) -> p s t", t=8), in_=ones_f[:].rearrange("p (s t) -> p s t", t=8), pattern=[[-8, 16], [0, 8]],
                                                     compare_op=ALU.is_ge, fill=0.0, base=0, channel_multiplier=1), reads=[ones_fb], writes=[mb])
            S.op("gpsimd", lambda e: e.affine_select(out=SS[:].rearrange("p (s t) -> p s t", t=8), in_=SS[:].rearrange("p (s t) -> p s t", t=8), pattern=[[8, 16], [0, 8]],
                                                     compare_op=ALU.is_ge, fill=0.0, base=7, channel_multiplier=-1), reads=[mb], writes=[mb])
            for nm in ("MUI", "MLS", "MUS"):
                S.op("gpsimd", lambda e, nm=nm: e.tensor_tensor(out=masks[(nm, "s")][:], in0=masks[(nm, "p")][:], in1=SS[:], op=ALU.mult), reads=[mb], writes=[mb])
            rowmask = sb("rowmask", [128, 16])
            S.op("gpsimd", lambda e: e.affine_select(out=rowmask[:], in_=ones_f[:, 0:16], pattern=[[-8, 16]], compare_op=ALU.is_ge, fill=0.0, base=0, channel_multiplier=1),
                 reads=[ones_fb], writes=[mb])
            S.op("gpsimd", lambda e: e.affine_select(out=rowmask[:], in_=rowmask[:], pattern=[[8, 16]], compare_op=ALU.is_ge, fill=0.0, base=7, channel_multiplier=-1),
                 reads=[mb], writes=[mb])
            ident_bf = sb("ident_bf", [128, 128], BF16)
            S.op("vector", lambda e: e.tensor_copy(out=ident_bf[:], in_=ident[:]), reads=[ident_b], writes=[mb])
            prm = sb("prm", [128, 160]); prmb = Buf("prm")
            S.dma("sync", prm[:, 0:8], wb_dt_bias[0, :].partition_broadcast(128), writes=[prmb])
            S.dma("sync", prm[:, 8:16], wb_a_log[0, :].partition_broadcast(128), writes=[prmb])
            S.dma("sync", prm[:, 32:160], wb_norm[0, :].partition_broadcast(128), writes=[prmb])
            S.op("scalar", lambda e: e.activation(out=prm[:, 16:24], in_=prm[:, 8:16], func=AF.Exp), reads=[prmb], writes=[prmb])
            S.op("vector", lambda e: e.tensor_scalar_mul(out=prm[:, 16:24], in0=prm[:, 16:24], scalar1=-1.0), reads=[prmb], writes=[prmb])
            HIST = sb("HIST", [128, 24, 3]); histb = [Buf("HIST%d" % c) for c in range(24)]
            S.op("gpsimd", lambda e: e.memset(HIST[:], 0.0), writes=histb)
            pre_bufs = {"p": [], "s": []}
            for (t_, _b) in chp.items:
                for q in range(3):
                    pre_bufs["p"].append((t_[:, q * 171:(q + 1) * 171], Buf("prep")))
                for q in range(2):
                    pre_bufs["s"].append((t_[:, q * 257:(q + 1) * 257], Buf("pres")))
            pre_rr = [0]

            pre_rr2 = [0, 0, 0]
            pre_extra = [(pre_extra_t[i], Buf("prex%d" % i)) for i in range(3)] if NPAR >= 3 else []
            ipb_rr = [0, 0, 0]

            def PRE(kind_, par_=0):
                lst = pre_bufs[kind_]
                if kind_ == "s":
                    it = lst[pre_rr[0] % len(lst)]
                    pre_rr[0] += 1
                    return it
                if par_ == 2:
                    it = pre_extra[pre_rr2[2] % 3]
                    pre_rr2[2] += 1
                    return it
                it = lst[par_ * 3 + pre_rr2[par_] % 3]
                pre_rr2[par_] += 1
                return it

            def ipbank(par_):
                if NPAR >= 3:
                    return ps[par_], ps_b[par_]
                i = par_ * 2 + ipb_rr[par_] % 2
                ipb_rr[par_] += 1
                return ps[i], ps_b[i]

            def miscbank(par_):
                if NPAR >= 3:
                    return ps[3 + par_], ps_b[3 + par_]
                return ps[4 + par_], ps_b[4 + par_]

            def chainbank(par_):
                if NPAR >= 3:
                    return ps[3 + par_], ps_b[3 + par_]
                return ps[6 + par_], ps_b[6 + par_]
            Sst = [res_f[7 + h] for h in range(8)]; sstb = [Buf("Sst%d" % h) for h in range(8)]
            stat3 = stat.items.pop()[0]
            Sbf = stat3[:, :].bitcast(BF16).rearrange("p (h v) -> p h v", v=128); sbfb = [Buf("Sbf%d" % h) for h in range(8)]
            for h in range(8):
                S.op("gpsimd", lambda e, h=h: e.memset(Sst[h][:], 0.0), writes=[sstb[h]])
            S.op("gpsimd", lambda e: e.memset(stat3[:, :], 0.0), writes=sbfb)
            sm = sb("sm", [128, 64]); smr = [0]
            ba_t = sb("ba_t", [128, 16]); ba_b = Buf("ba_t")
            GLs_t = hist_s[:, 0:4, :].rearrange("p c k -> p (c k)")
            PREh = chs[:].rearrange("p c s j -> p (c s j)")[:, 0:1152].rearrange("p (c k) -> p c k", k=48)

            smr2 = [0, 0, 0]

            def SM(w=8, par_=None):
                if par_ is None:
                    i = smr[0] % 8
                    smr[0] += 1
                    return sm[:, i * 8:i * 8 + w], smb[i]
                i = par_ * 2 + smr2[par_] % 2
                smr2[par_] += 1
                return sm[:, i * 8:i * 8 + w], smb[i]
            smb = [Buf("sm%d" % i) for i in range(8)]
            sm2 = sb("sm2", [128, 8, 24]); sm2b = Buf("sm2")
            presb = [Buf("pres%d" % c) for c in range(24)]
            for q in range(6):
                sct, sctb = xt_ring.next()
                S.dma("sync", sct[0:48, 0:512], scb[:, q * 512:(q + 1) * 512], writes=[sctb])
                pp, ppb = next_ps(4, 6)
                for c in range(4):
                    S.op("tensor", lambda e, c=c, pp=pp, sct=sct: e.transpose(pp[:, c * 48:(c + 1) * 48], sct[0:48, c * 128:(c + 1) * 128], ident[0:48, 0:48]),
                         reads=[sctb, ident_b], writes=[ppb])
                S.op("vector", lambda e, q=q, pp=pp: e.tensor_copy(out=PREh[:, q * 4:(q + 1) * 4, :], in_=pp[:, 0:192].rearrange("p (c k) -> p c k", k=48)),
                     reads=[ppb], writes=presb[q * 4:(q + 1) * 4])
            SINb = xt_ring.items[0][0][:, :].bitcast(BF16).rearrange("p (s v) -> p s v", v=128)
            sinbb = xt_ring.items[0][1]
            xt_ring.items.pop(0)
            ctx = dict(l=l, base=base, T=T, TB=TB, TI=TI, NPAR=NPAR, bufs_f1=bufs_f[1], bufs_x0=(bufs_b[2] + [b_ for (_t, b_) in pre_extra]), masks=masks, mb=mb, ones_f=ones_f, ones_fb=ones_fb, rowmask=rowmask, ident_bf=ident_bf,
                       prm=prm, prmb=prmb, HIST=HIST, histb=histb, PRE=PRE, ipbank=ipbank, miscbank=miscbank, chainbank=chainbank, Sst=Sst, sstb=sstb, Sbf=Sbf, sbfb=sbfb, SIN=SIN, sinb=sinb, SINb=SINb, sinbb=sinbb, WM=WM, ba_t=ba_t, ba_b=ba_b, GLs_t=GLs_t, PREh=PREh,
                       SM=SM, sm=sm, smb=smb, sm2=sm2, sm2b=sm2b, presb=presb)
            for ti in range(17):
                l1_tile(ti, ctx)
                ck(100 + ti)
            for q in range(6):
                pp, ppb = next_ps(4, 6)
                for c in range(4):
                    S.op("tensor", lambda e, c=c, q=q, pp=pp: e.transpose(pp[0:3, c * 128:(c + 1) * 128], HIST[:, q * 4 + c, :], ident[:]),
                         reads=[histb[q * 4 + c], ident_b], writes=[ppb])
                S.op("vector", lambda e, q=q, pp=pp: e.tensor_copy(out=small_o[0:3, (q % 2) * 512:(q % 2 + 1) * 512], in_=pp[0:3, :]), reads=[ppb], writes=[small_o_b])
                if q % 2 == 1:
                    S.dma("sync", cbp[:, (q // 2) * 1024:(q // 2 + 1) * 1024], small_o[0:3, :], reads=[small_o_b], is_output=True)
            for h in range(8):
                S.dma("sync", ssp[h], Sst[h][:], reads=[sstb[h]], is_output=True)
            for q in range(6):
                pp, ppb = next_ps(4, 6)
                for c in range(4):
                    S.op("tensor", lambda e, c=c, q=q, pp=pp: e.transpose(pp[0:48, c * 128:(c + 1) * 128], PREh[:, q * 4 + c, :], ident[:]),
                         reads=[presb[q * 4 + c], ident_b], writes=[ppb])
                S.op("vector", lambda e, q=q, pp=pp: e.tensor_copy(out=small_o[0:48, (q % 2) * 512:(q % 2 + 1) * 512], in_=pp[0:48, :]), reads=[ppb], writes=[small_o_b])
                if q % 2 == 1:
                    S.dma("sync", cbs[:, (q // 2) * 1024:(q // 2 + 1) * 1024], small_o[0:48, :], reads=[small_o_b], is_output=True)

        def l1_tile(ti, X):
            l = 1
            gi = X["base"] + ti
            kind, t0, n, col0 = groups[gi]
            T, TB, TI, masks, mb = X["T"], X["TB"], X["TI"], X["masks"], X["mb"]
            prm, prmb, SM = X["prm"], X["prmb"], X["SM"]
            MUI, MLS, MUS = masks[("MUI", kind)], masks[("MLS", kind)], masks[("MUS", kind)]
            ones_f, ones_fb = X["ones_f"], X["ones_fb"]
            make_u(l, gi)
            pb_, pbb = next_ps(0, 4)
            inproj(32, 128, pb_, pbb, mw=16)
            bafm, bafmb = T()
            S.op("vector", lambda e: e.tensor_copy(out=bafm[0:16, :], in_=pb_[0:16, 0:128]), reads=[pbb], writes=[bafmb])
            pq, pqb = next_ps(6, 8)
            S.op("tensor", lambda e: e.transpose(pq[:, 0:16], bafm[0:16, :], ident[0:16, 0:16]), reads=[bafmb, ident_b], writes=[pqb])
            sm2, sm2b = X["sm2"], X["sm2b"]
            ba, bab = X["ba_t"], X["ba_b"]
            S.op("vector", lambda e: e.tensor_copy(out=ba[:], in_=pq[:, 0:16]), reads=[pqb], writes=[bab])
            S.op("scalar", lambda e: e.activation(out=sm2[:, :, 0], in_=ba[:, 0:8], func=AF.Exp, scale=-1.0), reads=[bab], writes=[sm2b])
            S.op("vector", lambda e: e.tensor_scalar_add(out=sm2[:, :, 0], in0=sm2[:, :, 0], scalar1=1.0), reads=[sm2b], writes=[sm2b])
            S.op("scalar", lambda e: e.activation(out=sm2[:, :, 0], in_=sm2[:, :, 0], func=AF.Ln), reads=[sm2b], writes=[sm2b])
            S.op("scalar", lambda e: e.activation(out=sm2[:, :, 1], in_=sm2[:, :, 0], func=AF.Exp, scale=-0.5), reads=[sm2b], writes=[sm2b])
            S.op("scalar", lambda e: e.activation(out=sm2[:, :, 0], in_=sm2[:, :, 0], func=AF.Exp, scale=-1.0), reads=[sm2b], writes=[sm2b])
            S.op("vector", lambda e: e.tensor_tensor(out=sm2[:, :, 6], in0=ba[:, 8:16], in1=prm[:, 0:8], op=ALU.add), reads=[bab, prmb, sm2b], writes=[sm2b])
            S.op("vector", lambda e: e.tensor_scalar_min(out=sm2[:, :, 7], in0=sm2[:, :, 6], scalar1=0.0), reads=[sm2b], writes=[sm2b])
            S.op("vector", lambda e: e.tensor_scalar_max(out=ba[:, 0:8], in0=sm2[:, :, 6], scalar1=0.0), reads=[sm2b, bab], writes=[bab])
            S.op("vector", lambda e: e.tensor_tensor(out=sm2[:, :, 7], in0=ba[:, 0:8], in1=sm2[:, :, 7], op=ALU.subtract), reads=[sm2b, bab], writes=[sm2b])
            S.op("scalar", lambda e: e.activation(out=sm2[:, :, 7], in_=sm2[:, :, 7], func=AF.Exp, scale=-1.0), reads=[sm2b], writes=[sm2b])
            S.op("vector", lambda e: e.tensor_scalar_add(out=sm2[:, :, 7], in0=sm2[:, :, 7], scalar1=1.0), reads=[sm2b], writes=[sm2b])
            S.op("scalar", lambda e: e.activation(out=sm2[:, :, 7], in_=sm2[:, :, 7], func=AF.Ln), reads=[sm2b], writes=[sm2b])
            S.op("vector", lambda e: e.tensor_tensor(out=sm2[:, :, 6], in0=ba[:, 0:8], in1=sm2[:, :, 7], op=ALU.add), reads=[sm2b, bab], writes=[sm2b])
            S.op("vector", lambda e: e.tensor_tensor(out=sm2[:, :, 2], in0=sm2[:, :, 6], in1=prm[:, 16:24], op=ALU.mult), reads=[sm2b, prmb], writes=[sm2b])
            gq, gqb = X["sm"][:, (6 + ti % 2) * 8:(6 + ti % 2) * 8 + 8], X["smb"][6 + ti % 2]
            S.op("vector", lambda e: e.tensor_copy(out=gq, in_=sm2[:, :, 2]), reads=[sm2b], writes=[gqb])
            pg, pgb = next_ps(6, 8)
            S.op("tensor", lambda e: e.matmul(pg[:, 0:8], lhsT=MUI[:], rhs=gq, start=True, stop=True), reads=[mb, gqb], writes=[pgb])
            S.op("tensor", lambda e: e.matmul(pg[:, 8:16], lhsT=MLS[:], rhs=gq, start=True, stop=True), reads=[mb, gqb], writes=[pgb])
            S.op("vector", lambda e: e.tensor_copy(out=sm2[:, :, 3], in_=pg[:, 0:8]), reads=[pgb, sm2b], writes=[sm2b])
            S.op("scalar", lambda e: e.activation(out=sm2[:, :, 4], in_=pg[:, 0:8], func=AF.Exp), reads=[pgb, sm2b], writes=[sm2b])
            S.op("scalar", lambda e: e.activation(out=sm2[:, :, 5], in_=pg[:, 8:16], func=AF.Exp), reads=[pgb, sm2b], writes=[sm2b])
            if kind == "s":
                gr, grb = T()
                S.op("vector", lambda e: e.tensor_tensor(out=gr[:].rearrange("p (h s) -> p h s", s=16), in0=gq.unsqueeze(2).to_broadcast([128, 8, 16]),
                                                         in1=X["rowmask"][:].unsqueeze(1).to_broadcast([128, 8, 16]), op=ALU.mult), reads=[gqb, mb], writes=[grb])
                pgl, pglb = next_ps(6, 8)
                S.op("tensor", lambda e: e.matmul(pgl[:, 0:128], lhsT=ones_f[:], rhs=gr[:], start=True, stop=True), reads=[ones_fb, grb], writes=[pglb])
                GLs, GLsb = X["GLs_t"], Buf("GLs")
                S.op("scalar", lambda e: e.activation(out=GLs, in_=pgl[:, 0:128], func=AF.Exp), reads=[pglb], writes=[GLsb])
            Yd = dict(GLs=(GLs, GLsb) if kind == "s" else None)
            npar = X["NPAR"] if kind == "p" else 1
            if npar == 1:
                for h_ in range(8):
                    for _ in l1_head(ti, h_, X, gi, Yd, 0):
                        pass
            else:
                for h0 in range(0, 8, npar):
                    recs = []
                    for p_ in range(min(npar, 8 - h0)):
                        S.rec = []
                        for _ in l1_head(ti, h0 + p_, X, gi, Yd, p_):
                            pass
                        recs.append(S.rec)
                        S.rec = None
                    S.replay(recs)
            outproj_ln(l, gi, True)

        def l1_head(ti, h, X, gi, Y, par):
            l = 1
            kind, t0, n, col0 = groups[gi]
            masks, mb = X["masks"], X["mb"]
            T = lambda: X["T"](par)
            TB = lambda: X["TB"](par)
            TI = lambda: X["TI"](par)
            misc = lambda: X["miscbank"](par)
            prm, prmb, SM, sm2, sm2b = X["prm"], X["prmb"], X["SM"], X["sm2"], X["sm2b"]
            MUI, MLS, MUS = masks[("MUI", kind)], masks[("MLS", kind)], masks[("MUS", kind)]
            ones_f, ones_fb, ident_bf = X["ones_f"], X["ones_fb"], X["ident_bf"]
            HIST, histb, PREh, presb, WM = X["HIST"], X["histb"], X["PREh"], X["presb"], X["WM"]

            def col(k):
                return sm2[:, h, k:k + 1]
            fm = []
            st1 = []
            for j3 in range(3):
                chn = j3 * 8 + h
                pc, pcb = X["ipbank"](par)
                inproj(chn, 128, pc, pcb)
                acc, accb = T()
                pre, preb = X["PRE"](kind, par)
                if kind == "p":
                    S.op("gpsimd", lambda e, pre=pre, chn=chn: e.tensor_copy(out=pre[:, 0:3], in_=HIST[:, chn, :]), reads=[histb[chn]], writes=[preb])
                    S.op("scalar", lambda e, pre=pre, pc=pc: e.copy(out=pre[:, 3:131], in_=pc[:, 0:128]), reads=[pcb], writes=[preb])
                    S.op("gpsimd", lambda e, pre=pre, chn=chn: e.tensor_copy(out=HIST[:, chn, :], in_=pre[:, 128:131]), reads=[preb], writes=[histb[chn]])
                    taps = [pre[:, j:j + 128] for j in range(4)]
                    av = acc[:]
                else:
                    prv = pre[:, 0:176].rearrange("p (s j) -> p s j", j=11)
                    S.op("vector", lambda e, prv=prv, chn=chn: e.tensor_copy(out=prv[:, :, 0:3], in_=PREh[:, chn, :].rearrange("p (s j) -> p s j", j=3)),
                         reads=[presb[chn]], writes=[preb])
                    S.op("scalar", lambda e, pc=pc, prv=prv: e.copy(out=prv[:, :, 3:11], in_=pc[:, 0:128].rearrange("p (s t) -> p s t", t=8)),
                         reads=[pcb], writes=[preb])
                    S.op("vector", lambda e, prv=prv, chn=chn: e.tensor_copy(out=PREh[:, chn, :].rearrange("p (s j) -> p s j", j=3), in_=prv[:, :, 8:11]),
                         reads=[preb], writes=[presb[chn]])
                    taps = [prv[:, :, j:j + 8] for j in range(4)]
                    av = acc[:].rearrange("p (s t) -> p s t", t=8)
                S.op("scalar", lambda e, av=av, taps=taps, chn=chn: e.activation(out=av, in_=taps[0], func=AF.Identity, scale=wbconv(0, chn)), reads=[preb, pb_b], writes=[accb])
                st1.append((acc, accb, av, taps, chn, preb))
            pz, pzb = X["ipbank"](par)
            inproj(24 + h, 128, pz, pzb)
            sz, szb_ = T()
            S.op("scalar", lambda e: e.activation(out=sz[:], in_=pz[:, 0:128], func=AF.Silu), reads=[pzb], writes=[szb_])
            yield
            for (acc, accb, av, taps, chn, preb) in st1:
                for j in range(1, 4):
                    S.op("vector", lambda e, av=av, taps=taps, chn=chn, j=j: e.scalar_tensor_tensor(out=av, in0=taps[j], scalar=wbconv(j, chn), in1=av, op0=ALU.mult, op1=ALU.add),
                         reads=[preb, pb_b, accb], writes=[accb])
            yield
            for (acc, accb, av, taps, chn, preb) in st1:
                S.op("scalar", lambda e, acc=acc: e.activation(out=acc[:], in_=acc[:], func=AF.Silu), reads=[accb], writes=[accb])
                fm.append((acc, accb))
            (qT, qTb), (kT, kTb), (vT, vTb) = fm
            if ti == 0 and h == 0:
                dbg('qT', qT[:], qTb); dbg('kT', kT[:], kTb); dbg('vT', vT[:], vTb); dbg('sm2', sm2[:].rearrange('p h k -> p (h k)')[:, 0:128], sm2b)
            yield
            R, Rb = T()
            S.op("vector", lambda e: e.tensor_scalar_mul(out=R[:], in0=MUI[:], scalar1=col(2)), reads=[mb, sm2b], writes=[Rb])
            pG, pGb = misc()
            S.op("tensor", lambda e: e.matmul(pG[:, 0:128], lhsT=ones_f[:], rhs=R[:], start=True, stop=True), reads=[ones_fb, Rb], writes=[pGb])
            tA, tAb = T()
            S.op("vector", lambda e: e.tensor_scalar(out=tA[:], in0=pG[:, 0:128], scalar1=col(3), scalar2=0.0, op0=ALU.subtract, op1=ALU.max), reads=[pGb, sm2b], writes=[tAb])
            S.op("scalar", lambda e: e.activation(out=tA[:], in_=tA[:], func=AF.Exp, scale=-1.0), reads=[tAb], writes=[tAb])
            S.op("vector", lambda e: e.tensor_tensor(out=tA[:], in0=tA[:], in1=MLS[:], op=ALU.mult), reads=[tAb, mb], writes=[tAb])
            tB, tBb = T()
            S.op("vector", lambda e: e.tensor_scalar(out=tB[:], in0=pG[:, 0:128], scalar1=col(3), scalar2=0.0, op0=ALU.subtract, op1=ALU.min), reads=[pGb, sm2b], writes=[tBb])
            S.op("scalar", lambda e: e.activation(out=tB[:], in_=tB[:], func=AF.Exp), reads=[tBb], writes=[tBb])
            ETi, ETib = T()
            S.op("vector", lambda e: e.tensor_tensor(out=ETi[:], in0=tB[:], in1=MUI[:], op=ALU.mult), reads=[tBb, mb], writes=[ETib])
            S.op("vector", lambda e: e.tensor_tensor(out=tB[:], in0=tB[:], in1=MUS[:], op=ALU.mult), reads=[tBb, mb], writes=[tBb])
            if ti == 0 and h == 0:
                dbg('Em', tA[:], tAb); dbg('ETs', tB[:], tBb); dbg('ETi', ETi[:], ETib)
            eG, eGb = T()
            S.op("scalar", lambda e: e.activation(out=eG[:], in_=pG[:, 0:128], func=AF.Exp), reads=[pGb], writes=[eGb])
            yield
            pk, pkb = misc()
            S.op("tensor", lambda e: e.transpose(pk[:, 0:128], kT[:], ident[:]), reads=[kTb, ident_b], writes=[pkb])
            S.op("tensor", lambda e: e.transpose(pk[:, 128:256], vT[:], ident[:]), reads=[vTb, ident_b], writes=[pkb])
            sc, scb_ = SM(8, par)
            junk, junkb = T()
            S.op("scalar", lambda e: e.activation(out=junk[:], in_=pk[:, 0:128], func=AF.Square, accum_out=sc[:, 0:1]), reads=[pkb], writes=[junkb, scb_])
            S.op("vector", lambda e: e.tensor_scalar_add(out=sc[:, 0:1], in0=sc[:, 0:1], scalar1=NORM_EPS), reads=[scb_], writes=[scb_])
            S.op("scalar", lambda e: e.activation(out=sc[:, 0:1], in_=sc[:, 0:1], func=AF.Ln), reads=[scb_], writes=[scb_])
            S.op("scalar", lambda e: e.activation(out=sc[:, 0:1], in_=sc[:, 0:1], func=AF.Exp, scale=-0.5), reads=[scb_], writes=[scb_])
            S.op("vector", lambda e: e.tensor_tensor(out=sc[:, 1:2], in0=sc[:, 0:1], in1=col(1), op=ALU.mult), reads=[scb_, sm2b], writes=[scb_])
            kc, kcb = T()
            S.op("vector", lambda e: e.tensor_scalar_mul(out=kc[:], in0=pk[:, 0:128], scalar1=sc[:, 1:2]), reads=[pkb, scb_], writes=[kcb])
            if ti == 0 and h == 0:
                dbg('kc', kc[:], kcb); dbg('eG', eG[:], eGb)
            kcbf, kcbfb = TB()
            S.op("scalar", lambda e: e.copy(out=kcbf[:], in_=kc[:]), reads=[kcb], writes=[kcbfb])
            Rk, Rkb = TB()
            S.op("vector", lambda e: e.tensor_scalar_mul(out=Rk[:], in0=kc[:], scalar1=col(4)), reads=[kcb, sm2b], writes=[Rkb])
            ktc, ktcb = TB()
            S.op("vector", lambda e: e.tensor_scalar_mul(out=ktc[:], in0=kc[:], scalar1=col(5)), reads=[kcb, sm2b], writes=[ktcb])
            Rv, Rvb = TB()
            S.op("vector", lambda e: e.tensor_scalar_mul(out=Rv[:], in0=pk[:, 128:256], scalar1=col(1)), reads=[pkb, sm2b], writes=[Rvb])
            yield
            pt2, pt2b = misc()
            S.op("tensor", lambda e: e.transpose(pt2[:, 0:64].bitcast(BF16), kcbf[:], ident_bf[:]), reads=[kcbfb, mb], writes=[pt2b])
            kcT, kcTb = TB()
            S.op("vector", lambda e: e.tensor_copy(out=kcT[:], in_=pt2[:, 0:64].bitcast(BF16)), reads=[pt2b], writes=[kcTb])
            if ti == 0 and h == 0:
                dbg('kcT', kcT[:], kcTb); dbg('Rv', Rv[:], Rvb)
            qbf, qbfb = TB()
            S.op("scalar", lambda e: e.copy(out=qbf[:], in_=qT[:]), reads=[qTb], writes=[qbfb])
            qg, qgb = TB()
            S.op("vector", lambda e: e.tensor_tensor(out=qg[:], in0=qT[:], in1=eG[:], op=ALU.mult), reads=[qTb, eGb], writes=[qgb])
            qsq, qsqb = TB()
            S.op("scalar", lambda e: e.activation(out=qsq[:], in_=qT[:], func=AF.Square), reads=[qTb], writes=[qsqb])
            pqq, pqqb = misc()
            S.op("tensor", lambda e: e.matmul(pqq[:, 0:1], lhsT=qsq[:], rhs=ones_bf[:, 0:1], start=True, stop=True), reads=[qsqb, ones_b], writes=[pqqb])
            S.op("vector", lambda e: e.tensor_scalar_add(out=sc[:, 2:3], in0=pqq[:, 0:1], scalar1=NORM_EPS), reads=[pqqb, scb_], writes=[scb_])
            S.op("scalar", lambda e: e.activation(out=sc[:, 2:3], in_=sc[:, 2:3], func=AF.Ln), reads=[scb_], writes=[scb_])
            S.op("scalar", lambda e: e.activation(out=sc[:, 2:3], in_=sc[:, 2:3], func=AF.Exp, scale=-0.5), reads=[scb_], writes=[scb_])
            S.op("vector", lambda e: e.tensor_scalar_mul(out=sc[:, 3:4], in0=sc[:, 2:3], scalar1=128.0 ** -0.5), reads=[scb_], writes=[scb_])
            yield
            pkk, pkkb = misc()
            S.op("tensor", lambda e: e.matmul(pkk[:, 0:128], lhsT=kcT[:], rhs=kcT[:], start=True, stop=True), reads=[kcTb], writes=[pkkb])
            S.op("tensor", lambda e: e.matmul(pkk[:, 128:256], lhsT=kcT[:], rhs=qbf[:], start=True, stop=True), reads=[kcTb, qbfb], writes=[pkkb])
            USE_R = os.environ.get("MK_F32R", "0") == "1"
            rr = (lambda ap: ap.bitcast(F32R)) if USE_R else (lambda ap: ap)
            Xc, Xcb = TI()
            S.op("vector", lambda e, Xc=Xc: e.tensor_tensor(out=rr(Xc[:]), in0=pkk[:, 0:128], in1=tA[:], op=ALU.mult), reads=[pkkb, tAb], writes=[Xcb])
            Yc, Ycb = TI()
            S.op("vector", lambda e, Yc=Yc: e.tensor_tensor(out=rr(Yc[:]), in0=pkk[:, 0:128], in1=tB[:], op=ALU.mult), reads=[pkkb, tBb], writes=[Ycb])
            attnT, attnTb = TB()
            S.op("vector", lambda e: e.tensor_tensor(out=attnT[:], in0=pkk[:, 128:256], in1=ETi[:], op=ALU.mult), reads=[pkkb, ETib], writes=[attnTb])
            U, Ub = TI()
            S.op("vector", lambda e, U=U, Yc=Yc: e.tensor_tensor(out=rr(U[:]), in0=ident[:], in1=Yc[:], op=ALU.subtract), reads=[ident_b, mb, Ycb], writes=[Ub])
            yield
            nlev = 6 if kind == "p" else 2
            for j in range(1, nlev + 1):
                pl, plb = misc()
                S.op("tensor", lambda e, pl=pl, Xc=Xc, Yc=Yc: e.matmul(pl[:, 0:128], lhsT=rr(Yc[:]), rhs=rr(Xc[:]), start=True, stop=True), reads=[Xcb, Ycb], writes=[plb])
                if j < nlev:
                    S.op("tensor", lambda e, pl=pl, Xc=Xc, Yc=Yc: e.matmul(pl[:, 128:256], lhsT=rr(Xc[:]), rhs=rr(Yc[:]), start=True, stop=True), reads=[Xcb, Ycb], writes=[plb])
                Xn, Xnb = TI()
                S.op("scalar", lambda e, pl=pl, Xn=Xn: e.copy(out=rr(Xn[:]), in_=pl[:, 0:128]), reads=[plb], writes=[Xnb])
                if j < nlev:
                    Yn, Ynb = TI()
                    S.op("scalar", lambda e, pl=pl, Yn=Yn: e.copy(out=rr(Yn[:]), in_=pl[:, 128:256]), reads=[plb], writes=[Ynb])
                else:
                    Yn, Ynb = Yc, Ycb
                yield
                S.op("tensor", lambda e, pl=pl, Xn=Xn, U=U: e.matmul(pl[:, 256:384], lhsT=rr(Xn[:]), rhs=rr(U[:]), start=True, stop=True), reads=[Xnb, Ub], writes=[plb])
                Un, Unb = TI()
                S.op("vector", lambda e, pl=pl, Un=Un, U=U: e.tensor_tensor(out=rr(Un[:]), in0=pl[:, 256:384], in1=U[:], op=ALU.add), reads=[plb, Ub], writes=[Unb])
                Xc, Xcb, Yc, Ycb, U, Ub = Xn, Xnb, Yn, Ynb, Un, Unb
                yield
            if INV_DT == F32:
                Ubf, Ubfb = TB()
                S.op("scalar", lambda e, U=U: e.copy(out=Ubf[:], in_=U[:]), reads=[Ub], writes=[Ubfb])
            else:
                Ubf, Ubfb = U, Ub
            pu, pub = misc()
            S.op("tensor", lambda e: e.matmul(pu[:, 0:128], lhsT=Ubf[:], rhs=Rv[:], start=True, stop=True), reads=[Ubfb, Rvb], writes=[pub])
            S.op("tensor", lambda e: e.matmul(pu[:, 128:256], lhsT=Rk[:], rhs=Ubf[:], start=True, stop=True), reads=[Ubfb, Rkb], writes=[pub])
            up, upb = T()
            S.op("scalar", lambda e: e.copy(out=up[:], in_=pu[:, 0:128]), reads=[pub], writes=[upb])
            wT, wTb = TB()
            S.op("vector", lambda e: e.tensor_copy(out=wT[:], in_=pu[:, 128:256]), reads=[pub], writes=[wTb])
            yield
            if ti == 0 and h == 0:
                dbg('U', Ubf[:], Ubfb); dbg('up', up[:], upb); dbg('wT', wT[:], wTb)
            Sst, sstb, Sbf, sbfb = X["Sst"], X["sstb"], X["Sbf"], X["sbfb"]
            pc1, pc1b = X["chainbank"](par)
            pc2v, pc2b = pc1[:, 256:384], pc1b
            vn, vnb = TB()
            if kind == "p":
                S.op("tensor", lambda e: e.matmul(pc1[:, 0:128], lhsT=wT[:], rhs=Sbf[:, h, :], start=True, stop=True), reads=[wTb, sbfb[h]], writes=[pc1b])
                S.op("vector", lambda e: e.tensor_tensor(out=vn[:], in0=up[:], in1=pc1[:, 0:128], op=ALU.subtract), reads=[upb, pc1b], writes=[vnb])
                yield
                S.op("tensor", lambda e: e.matmul(pc2v, lhsT=qg[:], rhs=Sbf[:, h, :], start=True, stop=False), reads=[qgb, sbfb[h]], writes=[pc2b])
                S.op("tensor", lambda e: e.matmul(pc2v, lhsT=attnT[:], rhs=vn[:], start=False, stop=True), reads=[attnTb, vnb], writes=[pc2b])
                S.op("tensor", lambda e: e.matmul(pc1[:, 128:256], lhsT=ktc[:], rhs=vn[:], start=True, stop=True), reads=[ktcb, vnb], writes=[pc1b])
                S.op("vector", lambda e: e.tensor_copy(out=sc[:, 6:7], in_=eG[:, 127:128]), reads=[eGb, scb_], writes=[scb_])
                S.op("vector", lambda e: e.scalar_tensor_tensor(out=Sst[h][:], in0=Sst[h][:], scalar=sc[:, 6:7], in1=pc1[:, 128:256], op0=ALU.mult, op1=ALU.add),
                     reads=[sstb[h], scb_, pc1b], writes=[sstb[h]])
                S.op("scalar", lambda e: e.copy(out=Sbf[:, h, :], in_=Sst[h][:]), reads=[sstb[h]], writes=[sbfb[h]])
                if ti == 0 and h == 0:
                    dbg('vn', vn[:], vnb); dbg('S1', Sst[h][:], sstb[h])
            else:
                SIN, sinb, SINb, sinbb = X["SIN"], X["sinb"], X["SINb"], X["sinbb"]
                GLs, GLsb = Y["GLs"]
                rowmask = X["rowmask"]
                S.dma("sync", SIN[:], ssm[:, h].rearrange("s d v -> d s v"), writes=[sinb] + X["bufs_f1"])
                S.op("scalar", lambda e: e.copy(out=SINb[:], in_=SIN[:]), reads=[sinb], writes=[sinbb] + X["bufs_x0"])
                for s_ in range(16):
                    wm_, wmb_ = WM()
                    S.op("gpsimd", lambda e, wm_=wm_: e.memset(wm_[:], 0.0), writes=[wmb_])
                    S.op("vector", lambda e, wm_=wm_, s_=s_: e.tensor_copy(out=wm_[:, s_ * 8:s_ * 8 + 8], in_=wT[:, s_ * 8:s_ * 8 + 8]), reads=[wTb], writes=[wmb_])
                    S.op("tensor", lambda e, wm_=wm_, s_=s_: e.matmul(pc1[:, 0:128], lhsT=wm_[:], rhs=SINb[:, s_, :], start=(s_ == 0), stop=(s_ == 15)),
                         reads=[wmb_, sinbb], writes=[pc1b])
                S.op("vector", lambda e: e.tensor_tensor(out=vn[:], in0=up[:], in1=pc1[:, 0:128], op=ALU.subtract), reads=[upb, pc1b], writes=[vnb])
                for s_ in range(16):
                    wm_, wmb_ = WM()
                    S.op("gpsimd", lambda e, wm_=wm_: e.memset(wm_[:], 0.0), writes=[wmb_])
                    S.op("vector", lambda e, wm_=wm_, s_=s_: e.tensor_copy(out=wm_[:, s_ * 8:s_ * 8 + 8], in_=qg[:, s_ * 8:s_ * 8 + 8]), reads=[qgb], writes=[wmb_])
                    S.op("tensor", lambda e, wm_=wm_, s_=s_: e.matmul(pc2v, lhsT=wm_[:], rhs=SINb[:, s_, :], start=(s_ == 0), stop=False),
                         reads=[wmb_, sinbb], writes=[pc2b])
                S.op("tensor", lambda e: e.matmul(pc2v, lhsT=attnT[:], rhs=vn[:], start=False, stop=True), reads=[attnTb, vnb], writes=[pc2b])
                for s_ in range(16):
                    wm_, wmb_ = WM()
                    S.op("vector", lambda e, wm_=wm_, s_=s_: e.tensor_scalar_mul(out=wm_[:], in0=ktc[:], scalar1=rowmask[:, s_:s_ + 1]), reads=[ktcb, mb], writes=[wmb_])
                    pss, pssb = misc()
                    S.op("tensor", lambda e, wm_=wm_, pss=pss: e.matmul(pss[:, 0:128], lhsT=wm_[:], rhs=vn[:], start=True, stop=True), reads=[wmb_, vnb], writes=[pssb])
                    S.op("vector", lambda e, s_=s_, pss=pss: e.scalar_tensor_tensor(out=SIN[:, s_, :], in0=SIN[:, s_, :], scalar=GLs[:, h * 16 + s_:h * 16 + s_ + 1], in1=pss[:, 0:128],
                                                                                      op0=ALU.mult, op1=ALU.add), reads=[sinb, GLsb, pssb], writes=[sinb])
                S.dma("sync", sss[:, h].rearrange("s d v -> d s v"), SIN[:], reads=[sinb], is_output=True)
            yield
            S.op("scalar", lambda e: e.activation(out=junk[:], in_=pc2v, func=AF.Square, accum_out=sc[:, 4:5]), reads=[pc2b, junkb], writes=[junkb, scb_])
            S.op("vector", lambda e: e.tensor_tensor(out=sc[:, 7:8], in0=sc[:, 3:4], in1=sc[:, 3:4], op=ALU.mult), reads=[scb_], writes=[scb_])
            S.op("vector", lambda e: e.tensor_tensor(out=sc[:, 4:5], in0=sc[:, 4:5], in1=sc[:, 7:8], op=ALU.mult), reads=[scb_], writes=[scb_])
            S.op("vector", lambda e: e.tensor_scalar(out=sc[:, 4:5], in0=sc[:, 4:5], scalar1=1.0 / 128.0, scalar2=NORM_EPS, op0=ALU.mult, op1=ALU.add), reads=[scb_], writes=[scb_])
            S.op("scalar", lambda e: e.activation(out=sc[:, 4:5], in_=sc[:, 4:5], func=AF.Ln), reads=[scb_], writes=[scb_])
            S.op("scalar", lambda e: e.activation(out=sc[:, 4:5], in_=sc[:, 4:5], func=AF.Exp, scale=-0.5), reads=[scb_], writes=[scb_])
            S.op("vector", lambda e: e.tensor_tensor(out=sc[:, 5:6], in0=sc[:, 4:5], in1=sc[:, 3:4], op=ALU.mult), reads=[scb_], writes=[scb_])
            on, onb = T()
            S.op("vector", lambda e: e.scalar_tensor_tensor(out=on[:], in0=pc2v, scalar=sc[:, 5:6], in1=prm[:, 32:160], op0=ALU.mult, op1=ALU.mult),
                 reads=[pc2b, scb_, prmb], writes=[onb])
            if ti == 0 and h == 0:
                dbg('on', on[:], onb); dbg('sc', sc, scb_, ncols=8)
            po, pob = misc()
            S.op("tensor", lambda e: e.transpose(po[:, 0:128], on[:], ident[:]), reads=[onb, ident_b], writes=[pob])
            S.op("vector", lambda e: e.tensor_tensor(out=YT[:, h, 0:128], in0=po[:, 0:128], in1=sz[:], op=ALU.mult), reads=[pob, szb_], writes=[yt_b])


        try:
            layer0()
            layer0_tail()
            if depth_run >= 2:
                layer1()
        except _Stop:
            pass
        S.finish("sync")
        S.emit(block)
        build.stats = S.stats()
    return nc


_NC_CACHE = {}


def kernel(x_prompt, x_sample, state_conv_a, state_conv_b, state_ssm_b, c_prompt, c_sample,
           w_mod, b_mod, ln_g, ln_b, wa_in, wa_conv, wa_out,
           wb_in, wb_conv, wb_a_log, wb_dt_bias, wb_norm, wb_out, _depth_run=2):
    f = lambda a: np.ascontiguousarray(np.asarray(a, dtype=np.float32))
    x_prompt, x_sample = f(x_prompt), f(x_sample)
    state_conv_a, state_conv_b, state_ssm_b = f(state_conv_a), f(state_conv_b), f(state_ssm_b)
    c_prompt, c_sample = f(c_prompt), f(c_sample)
    shared = {
        "w_mod": f(w_mod), "b_mod": f(b_mod), "ln_g": f(ln_g), "ln_b": f(ln_b),
        "wa_in": f(wa_in)[0], "wa_conv": f(wa_conv)[0], "wa_out": f(wa_out)[0],
        "wb_in": f(wb_in)[0], "wb_conv": f(wb_conv)[0], "wb_a_log": f(wb_a_log), "wb_dt_bias": f(wb_dt_bias),
        "wb_norm": f(wb_norm), "wb_out": f(wb_out)[0],
    }
    in_maps = []
    for i in range(NCORES):
        sl = slice(i * 16, (i + 1) * 16)
        m = dict(shared)
        m["xp"] = x_prompt[i]
        m["xs"] = x_sample[sl].reshape(NT_S, D)
        m["cc"] = np.concatenate([c_prompt[i:i + 1], c_sample[sl]], axis=0)
        m["sca"] = state_conv_a[0, sl].reshape(32, D)
        m["scb"] = state_conv_b[0, sl].reshape(48, 3072)
        m["ssm"] = state_ssm_b[0, sl]
        in_maps.append(m)
    if _depth_run not in _NC_CACHE:
        _NC_CACHE[_depth_run] = build(_depth_run)
    nc = _NC_CACHE[_depth_run]
    res = run_bass_kernel_spmd(nc, in_maps, core_ids=list(range(NCORES)))
    R = res.results
    kernel.last_results = R
    y_prompt = np.stack([R[i]["yp"] for i in range(NCORES)], axis=0)
    y_sample = np.concatenate([R[i]["ys"].reshape(16, 8, D) for i in range(NCORES)], axis=0)
    conv_a_p = np.stack([R[i]["cap"] for i in range(NCORES)], axis=0)[None]
    conv_b_p = np.stack([R[i]["cbp"] for i in range(NCORES)], axis=0)[None]
    ssm_p = np.stack([R[i]["ssp"] for i in range(NCORES)], axis=0)[None]
    conv_a_s = np.concatenate([R[i]["cas"].reshape(16, 2, D) for i in range(NCORES)], axis=0)[None]
    conv_b_s = np.concatenate([R[i]["cbs"].reshape(16, 3, 3072) for i in range(NCORES)], axis=0)[None]
    ssm_s = np.concatenate([R[i]["sss"] for i in range(NCORES)], axis=0)[None]
    return (y_prompt, y_sample, conv_a_p, conv_b_p, ssm_p, conv_a_s, conv_b_s, ssm_s)
```
